# Optimizing a Trainium2 kernel written in Bass

```python
import jax, jax.numpy as jnp
from jax import lax
import numpy as np

D_MODEL = 1024
BATCH = 8
SEQ = 8192
DEPTH = 1

N_META = 16
RET_HEADS = 4
RET_QK_DIM = D_MODEL // RET_HEADS
RET_WIDTH = 2 * D_MODEL
RET_V_DIM = RET_WIDTH // RET_HEADS
RET_CHUNK = 64
GLA_HEADS = 4
GLA_K_DIM = (D_MODEL // 2) // GLA_HEADS
GLA_WIDTH = D_MODEL
GLA_V_DIM = GLA_WIDTH // GLA_HEADS
GLA_GATE_RANK = 16
GLA_GATE_TAU = 16.0
GLA_CHUNK = 16
ROPE_BASE = 10000.0
EPS = 1e-6
IN_SIZES = (RET_HEADS * RET_QK_DIM, RET_HEADS * RET_QK_DIM, RET_WIDTH, RET_WIDTH,
            GLA_HEADS * GLA_K_DIM, GLA_HEADS * GLA_K_DIM, GLA_WIDTH, GLA_WIDTH,
            GLA_GATE_RANK, D_MODEL, D_MODEL)
IN_COLS = sum(IN_SIZES)

kernel_name = "hybrid_retention_gla_gated_merge"


def rms_norm(x, gain):
    xf = x.astype(jnp.float32)
    y = xf * lax.rsqrt(jnp.mean(xf * xf, axis=-1, keepdims=True) + EPS) * gain.astype(jnp.float32)
    return y.astype(x.dtype)


def head_group_norm(o, gain):
    of = o.astype(jnp.float32)
    mu = jnp.mean(of, axis=-1, keepdims=True)
    var = jnp.mean(jnp.square(of - mu), axis=-1, keepdims=True)
    return ((of - mu) * lax.rsqrt(var + EPS) * gain.astype(jnp.float32)).astype(o.dtype)


def head_rms_norm(o, gain):
    of = o.astype(jnp.float32)
    return (of * lax.rsqrt(jnp.mean(of * of, axis=-1, keepdims=True) + EPS) * gain.astype(jnp.float32)).astype(o.dtype)


def rope(t, pos):
    half = t.shape[-1] // 2
    inv = ROPE_BASE ** (-jnp.arange(half, dtype=jnp.float32) / half)
    ang = pos[:, None] * inv[None, :]
    cos = jnp.cos(ang)[None, :, None, :]
    sin = jnp.sin(ang)[None, :, None, :]
    t1 = t[..., :half].astype(jnp.float32)
    t2 = t[..., half:].astype(jnp.float32)
    return jnp.concatenate([t1 * cos - t2 * sin, t2 * cos + t1 * sin], axis=-1).astype(t.dtype)


def to_chunks(t, c):
    pad = (-N_META) % c
    t = jnp.pad(t, ((0, 0), (pad, 0), (0, 0), (0, 0)))
    b, lp, h, d = t.shape
    return t.reshape(b, lp // c, c, h, d)


def from_chunks(t, c):
    b, n, _, h, d = t.shape
    pad = (-N_META) % c
    return t.reshape(b, n * c, h, d)[:, pad:]


def retention_chunked(q, k, v):
    c = RET_CHUNK
    qc, kc, vc = to_chunks(q, c), to_chunks(k, c), to_chunks(v, c)
    log_gamma = jnp.log1p(-(2.0 ** (-5.0 - jnp.arange(RET_HEADS, dtype=jnp.float32))))
    idx = jnp.arange(c, dtype=jnp.float32)
    rel = idx[:, None] - idx[None, :]
    decay = jnp.where(rel[None] >= 0, jnp.exp(jnp.maximum(rel, 0.0)[None] * log_gamma[:, None, None]), 0.0)
    scores = jnp.einsum('bnihd,bnjhd->bnhij', qc, kc) * decay[None, None]
    intra = jnp.einsum('bnhij,bnjhe->bnihe', scores, vc)
    xi = jnp.exp((idx[:, None] + 1.0) * log_gamma[None, :])
    zeta = jnp.exp((c - 1.0 - idx[:, None]) * log_gamma[None, :])
    gamma_c = jnp.exp(c * log_gamma)

    def step(state, xs):
        q_n, k_n, v_n = xs
        inter = jnp.einsum('bihd,bhde->bihe', q_n, state) * xi[None, :, :, None]
        state = state * gamma_c[None, :, None, None] + jnp.einsum('bjhd,bjhe->bhde', k_n * zeta[None, :, :, None], v_n)
        return state, inter

    bsz = q.shape[0]
    state0 = jnp.zeros((bsz, RET_HEADS, RET_QK_DIM, RET_V_DIM), jnp.float32)
    xs = (jnp.moveaxis(qc, 1, 0), jnp.moveaxis(kc, 1, 0), jnp.moveaxis(vc, 1, 0))
    _, inter = lax.scan(step, state0, xs)
    out = intra + jnp.moveaxis(inter, 0, 1)
    return from_chunks(out, c).astype(v.dtype)


def gla_chunked(q, k, v, log_a):
    c = GLA_CHUNK
    qc, kc, vc = to_chunks(q, c), to_chunks(k, c), to_chunks(v, c)
    ac = to_chunks(log_a.astype(jnp.float32), c)
    b = jnp.cumsum(ac, axis=2)
    b_last = b[:, :, -1]
    q_dec = qc * jnp.exp(b)
    k_inv = kc * jnp.exp(-b)
    k_end = kc * jnp.exp(b_last[:, :, None] - b)
    mask = jnp.tril(jnp.ones((c, c), dtype=bool))
    scores = jnp.where(mask, jnp.einsum('bnihd,bnjhd->bnhij', q_dec, k_inv), 0.0)
    intra = jnp.einsum('bnhij,bnjhe->bnihe', scores, vc)

    def step(state, xs):
        q_n, k_n, v_n, a_n = xs
        inter = jnp.einsum('bihd,bhde->bihe', q_n, state)
        state = state * jnp.exp(a_n)[..., None] + jnp.einsum('bjhd,bjhe->bhde', k_n, v_n)
        return state, inter

    bsz = q.shape[0]
    state0 = jnp.zeros((bsz, GLA_HEADS, GLA_K_DIM, GLA_V_DIM), jnp.float32)
    xs = (jnp.moveaxis(q_dec, 1, 0), jnp.moveaxis(k_end, 1, 0), jnp.moveaxis(vc, 1, 0), jnp.moveaxis(b_last, 1, 0))
    _, inter = lax.scan(step, state0, xs)
    out = intra + jnp.moveaxis(inter, 0, 1)
    return from_chunks(out, c).astype(v.dtype)


def hybrid_layer(h, norm_gain, w_in, w_gate_up, b_gate, ret_norm_gain, gla_norm_gain,
                 w_branch_ret, w_branch_gla, w_out):
    bsz, length, _ = h.shape
    u = rms_norm(h, norm_gain)
    proj = u @ w_in
    points = [int(p) for p in np.cumsum(IN_SIZES)[:-1]]
    (rq, rk, rv, rg, gq, gk, gv, gg, glr, m_ret, m_gla) = jnp.split(proj, points, axis=-1)

    pos = jnp.arange(length, dtype=jnp.float32)
    rq = rope(rq.reshape(bsz, length, RET_HEADS, RET_QK_DIM), pos)
    rk = rope(rk.reshape(bsz, length, RET_HEADS, RET_QK_DIM), pos) * (RET_QK_DIM ** -0.5)
    rv = rv.reshape(bsz, length, RET_HEADS, RET_V_DIM)
    o_ret = retention_chunked(rq, rk, rv)
    o_ret = head_group_norm(o_ret, ret_norm_gain.reshape(RET_HEADS, RET_V_DIM)).reshape(bsz, length, RET_WIDTH)
    o_ret = o_ret * jax.nn.silu(rg)

    gq = gq.reshape(bsz, length, GLA_HEADS, GLA_K_DIM) * (GLA_K_DIM ** -0.5)
    gk = gk.reshape(bsz, length, GLA_HEADS, GLA_K_DIM)
    gv = gv.reshape(bsz, length, GLA_HEADS, GLA_V_DIM)
    log_a = jax.nn.log_sigmoid((glr @ w_gate_up + b_gate).astype(jnp.float32)) / GLA_GATE_TAU
    log_a = log_a.reshape(bsz, length, GLA_HEADS, GLA_K_DIM)
    o_gla = gla_chunked(gq, gk, gv, log_a)
    o_gla = head_rms_norm(o_gla, gla_norm_gain.reshape(GLA_HEADS, GLA_V_DIM)).reshape(bsz, length, GLA_WIDTH)
    o_gla = o_gla * jax.nn.silu(gg)

    merged = jax.nn.sigmoid(m_ret) * (o_ret @ w_branch_ret) + jax.nn.sigmoid(m_gla) * (o_gla @ w_branch_gla)
    return h + merged @ w_out


def setup_inputs(seed: int = 0) -> dict:
    key = jax.random.key(seed)
    ks = jax.random.split(key, 12)
    f = jnp.float32
    gk_dim = GLA_HEADS * GLA_K_DIM
    return {
        "x": jax.random.normal(ks[0], (BATCH, SEQ, D_MODEL), f),
        "meta_tokens": jax.random.normal(ks[1], (N_META, D_MODEL), f),
        "norm_gain": 1.0 + 0.02 * jax.random.normal(ks[2], (DEPTH, D_MODEL), f),
        "w_in": jax.random.normal(ks[3], (DEPTH, D_MODEL, IN_COLS), f) * D_MODEL ** -0.5,
        "w_gate_up": jax.random.normal(ks[4], (DEPTH, GLA_GATE_RANK, gk_dim), f) * GLA_GATE_RANK ** -0.5,
        "b_gate": 0.01 * jax.random.normal(ks[5], (DEPTH, gk_dim), f),
        "ret_norm_gain": 1.0 + 0.02 * jax.random.normal(ks[6], (DEPTH, RET_WIDTH), f),
        "gla_norm_gain": 1.0 + 0.02 * jax.random.normal(ks[7], (DEPTH, GLA_WIDTH), f),
        "w_branch_ret": jax.random.normal(ks[8], (DEPTH, RET_WIDTH, D_MODEL), f) * RET_WIDTH ** -0.5,
        "w_branch_gla": jax.random.normal(ks[9], (DEPTH, GLA_WIDTH, D_MODEL), f) * GLA_WIDTH ** -0.5,
        "w_out": jax.random.normal(ks[10], (DEPTH, D_MODEL, D_MODEL), f) * D_MODEL ** -0.5,
        "final_norm_gain": 1.0 + 0.02 * jax.random.normal(ks[11], (D_MODEL,), f),
    }


def reference(x, meta_tokens, norm_gain, w_in, w_gate_up, b_gate, ret_norm_gain, gla_norm_gain,
              w_branch_ret, w_branch_gla, w_out, final_norm_gain):
    bsz = x.shape[0]
    meta = jnp.broadcast_to(meta_tokens.astype(x.dtype)[None], (bsz, N_META, D_MODEL))
    h = jnp.concatenate([meta, x], axis=1)
    for layer in range(DEPTH):
        h = hybrid_layer(h, norm_gain[layer], w_in[layer], w_gate_up[layer], b_gate[layer],
                         ret_norm_gain[layer], gla_norm_gain[layer], w_branch_ret[layer],
                         w_branch_gla[layer], w_out[layer])
    h = rms_norm(h, final_norm_gain)
    return h[:, N_META:]
```

```python
import math
import numpy as np
import ml_dtypes
import concourse.bass as bass
import concourse.mybir as mybir
from concourse.bass_utils import run_bass_kernel_spmd

F32 = mybir.dt.float32
BF16 = mybir.dt.bfloat16
AF = mybir.ActivationFunctionType
ALU = mybir.AluOpType

D = 1024
KC = 8
N_META = 16
IN_COLS = 11280
HR, DKR, DVR = 4, 256, 512
HG, DKG, DVG = 4, 128, 256
EPS = 1e-6
TAU = 16.0
O_RQ, O_RK, O_RV, O_RG = 0, 1024, 2048, 4096
O_GQ, O_GK, O_GV, O_GG = 6144, 6656, 7168, 8192
O_GLR, O_MR, O_MG = 9216, 9232, 10256
GAM = [1.0 - 2.0 ** (-5.0 - h) for h in range(HR)]
LOGG = [math.log1p(-(2.0 ** (-5.0 - h))) for h in range(HR)]
GAMC = [math.exp(128.0 * lg) for lg in LOGG]
NS = 3


class View:
    __slots__ = ("ap", "name", "rngs")

    def __init__(self, ap, name, rngs):
        self.ap, self.name, self.rngs = ap, name, rngs


class Buf:
    def __init__(self, name, ap, shape, base=0, gran=1):
        self.name, self.ap, self.shape, self.base = name, ap, list(shape), base
        self.gran = gran
        self.f = self.shape[1:]
        st = [1] * len(self.f)
        for i in range(len(self.f) - 2, -1, -1):
            st[i] = st[i + 1] * self.f[i + 1]
        self.st = st

    def __call__(self, *idx, p=None):
        idx = list(idx) + [slice(None)] * (len(self.f) - len(idx))
        key = [slice(None) if p is None else slice(p[0], p[1])] + idx
        ap = self.ap[tuple(key)]
        dims = []
        for n, ix in zip(self.f, idx):
            if isinstance(ix, int):
                dims.append((ix, ix + 1))
            else:
                lo = 0 if ix.start is None else ix.start
                hi = n if ix.stop is None else ix.stop
                dims.append((lo, hi))
        k = len(dims) - 1
        while k > 0 and dims[k] == (0, self.f[k]):
            k -= 1
        rngs = []

        def rec(i, off):
            if i == k:
                rngs.append((off + dims[k][0] * self.st[k], off + dims[k][1] * self.st[k]))
                return
            for v in range(dims[i][0], dims[i][1]):
                rec(i + 1, off + v * self.st[i])

        rec(0, self.base)
        rs_ = getattr(self, "rscale", 1)
        if rs_ > 1:
            rngs = [(lo // rs_, -((-hi) // rs_)) for lo, hi in rngs]
        g = self.gran // rs_ if rs_ > 1 else self.gran
        if g > 1:
            rngs = [((lo // g) * g, -((-hi) // g) * g) for lo, hi in rngs]
        rngs.sort()
        m = [list(rngs[0])]
        for lo, hi in rngs[1:]:
            if lo <= m[-1][1]:
                m[-1][1] = max(m[-1][1], hi)
            else:
                m.append([lo, hi])
        return View(ap, self.name, [tuple(x) for x in m])


def raw(ap, name, lo, hi):
    return View(ap, name, [(lo, hi)])


class Sched:
    QUEUES = ["pe", "act", "dve", "pool", "sp"]

    def __init__(self):
        self.prog = {q: [] for q in self.QUEUES}
        self.cnt = {}
        self.waited = {q: {} for q in self.QUEUES}
        self.recs = {}
        self.n_ops = 0
        self.tag = ""

    PSUM_NAMES = ("pj", "sc", "oacc", "stp", "ptr")

    def _deps(self, v, write):
        out = []
        recs = self.recs.get(v.name, [])
        excl = v.name in self.PSUM_NAMES
        for lo, hi in v.rngs:
            for r in recs:
                if r[0] < hi and lo < r[1]:
                    if r[2] is not None:
                        out.append((r[2], "waw" if write else "raw"))
                    if write or excl:
                        for sem, (val, eng) in r[3].items():
                            out.append(((sem, val, eng), "war"))
        return out

    def _note(self, v, write, ev):
        recs = self.recs.setdefault(v.name, [])
        for lo, hi in v.rngs:
            if write:
                recs[:] = [r for r in recs if not (lo <= r[0] and r[1] <= hi)]
                recs.append([lo, hi, ev, {}])
            else:
                hit = None
                for r in recs:
                    if r[0] == lo and r[1] == hi:
                        hit = r
                        break
                if hit is None:
                    hit = [lo, hi, None, {}]
                    recs.append(hit)
                hit[3][ev[0]] = (ev[1], ev[2])

    def op(self, q, fn, R=(), W=(), dma=None):
        self.n_ops += 1
        if dma is not None:
            sem, inc, eng = dma, 16, "dma"
        else:
            sem, inc, eng = q, 1, q
        self.cnt[sem] = self.cnt.get(sem, 0) + inc
        ev = (sem, self.cnt[sem], eng)
        need = {}
        for v in R:
            for (s, val, e), kind in self._deps(v, False):
                if e == q and (q == "pe" or kind == "war"):
                    continue
                need[s] = max(need.get(s, 0), val)
        for v in W:
            for (s, val, e), kind in self._deps(v, True):
                if e == q and q == "pe":
                    continue
                need[s] = max(need.get(s, 0), val)
        waits = []
        wq = self.waited[q]
        for s, val in need.items():
            if wq.get(s, 0) >= val:
                continue
            wq[s] = val
            waits.append((s, val))
        self.prog[q].append((waits, fn, sem, inc, self.tag))
        for v in R:
            self._note(v, False, ev)
        for v in W:
            self._note(v, True, ev)
        return ev


def build_program(n_xtiles):
    assert n_xtiles % 4 == 0
    n_tiles = n_xtiles + 1
    blocks = [(0, 1)] + [(1 + 4 * i, 4) for i in range(n_xtiles // 4)]
    nblk = len(blocks)
    LP = n_tiles * 128

    nc = bass.Bass("TRN2", target_bir_lowering=False)
    dt_in = {}

    def din(name, shape, dt=F32):
        dt_in[name] = nc.dram_tensor(name, list(shape), dt, kind="ExternalInput").ap()
        return dt_in[name]

    x_d = din("x", [n_xtiles * 128, D])
    meta_d = din("meta", [N_META, D])
    win_d = din("w_in", [D, IN_COLS])
    wbr_d = din("w_bret", [2048, D])
    wbg_d = din("w_bgla", [D, D])
    wout_d = din("w_out", [D, D])
    wg_d = din("w_gate", [16, 512])
    gin_d = din("g_in", [128, 8])
    gret_d = din("g_ret", [128, 16])
    ggla_d = din("g_gla", [128, 8])
    bg_d = din("b_gate", [128, 4])
    fg_d = din("fgain", [128, D])
    cs_d = din("cs", [128, 2, LP])
    ident_d = din("ident", [128, 128], BF16)
    dmask_d = din("dmask", [128, 4, 128])
    causal_d = din("causal", [128, 128])
    xi_d = din("xi", [128, 4, 128])
    zeta_d = din("zeta", [128, 4])
    out_d = nc.dram_tensor("out", [n_xtiles * 128, D], F32, kind="ExternalOutput").ap()

    chunks = []

    def win_chunk(colsegs, gain="in"):
        w = sum(n for _, n in colsegs)
        pieces = []
        dst = 0
        for c0, n in colsegs:
            nkc = max(1, min(8, 1024 // n))
            for kc0 in range(0, 8, nkc):
                pieces.append(("win", c0, n, kc0, nkc, dst))
            dst += n
        chunks.append(dict(kc=8, w=w, pieces=pieces, gain=gain))
        return len(chunks) - 1

    C = {}
    for h in range(HR):
        C["qk", h] = win_chunk([(O_RQ + h * 256, 256), (O_RK + h * 256, 256)])
        C["v", h] = win_chunk([(O_RV + h * 512, 512)])
        C["g", h] = win_chunk([(O_RG + h * 512, 512)])
    C["glr"] = win_chunk([(O_GLR, 16)])
    for hp in range(2):
        g0, g1 = 2 * hp, 2 * hp + 1
        C["gqk", hp] = win_chunk([(O_GQ + g0 * 128, 128), (O_GQ + g1 * 128, 128),
                                  (O_GK + g0 * 128, 128), (O_GK + g1 * 128, 128)])
        C["gv", hp] = win_chunk([(O_GV + hp * 512, 512)])
        C["gg", hp] = win_chunk([(O_GG + hp * 512, 512)])
    for mh in range(2):
        C["mr", mh] = win_chunk([(O_MR + mh * 512, 512)])
        C["mg", mh] = win_chunk([(O_MG + mh * 512, 512)])
        for j in range(2):
            n0 = mh * 512 + j * 256
            chunks.append(dict(kc=16, w=256, gain="ret",
                               pieces=[("wbr", n0, 256, kc0, 4, 0) for kc0 in range(0, 16, 4)]))
            C["br", mh, j] = len(chunks) - 1
        chunks.append(dict(kc=8, w=512, gain="gla",
                           pieces=[("wbg", mh * 512, 512, kc0, 2, 0) for kc0 in range(0, 8, 2)]))
        C["bg", mh] = len(chunks) - 1
    for hf in range(2):
        chunks.append(dict(kc=8, w=512, gain=None,
                           pieces=[("wout", hf * 512, 512, kc0, 2, 0) for kc0 in range(0, 8, 2)]))
        C["out", hf] = len(chunks) - 1
    NCH = len(chunks)
    order = []
    for h in range(HR):
        order += [("qk", h), ("v", h), ("g", h)]
    order += ["glr"]
    for hp in range(2):
        order += [("gv", hp), ("gg", hp), ("gqk", hp)]
    order += [("mr", 0), ("mg", 0), ("mr", 1), ("mg", 1)]
    for mh in range(2):
        order += [("br", mh, 0), ("br", mh, 1), ("bg", mh)]
    order += [("out", 0), ("out", 1)]
    assert len(order) == NCH and len(set(order)) == NCH
    order_c = [C[k] for k in order]
    pos_of = {c: i for i, c in enumerate(order_c)}
    scr_d = nc.dram_tensor("wscr", [NCH, 128, 4096], BF16, kind="Internal").ap()

    S = Sched()
    ctx = {}

    import contextlib
    with contextlib.ExitStack() as es:
        def sb(name, shape, dt):
            t = es.enter_context(nc.sbuf_tensor("sb_" + name, list(shape), dt))
            return t

        def ps(name, shape, dt):
            t = es.enter_context(nc.psum_tensor("ps_" + name, list(shape), dt))
            return t

        def mk(name, shape, dt, psum=False, gran=1):
            t = (ps if psum else sb)(name, shape, dt)
            key = tuple([slice(None)] * len(shape))
            return Buf(name, t[key], shape, gran=gran)

        ident = mk("ident", [128, 128], BF16)
        dmask = mk("dmask", [128, 4, 128], F32)
        causal = mk("causal", [128, 128], F32)
        xi = mk("xi", [128, 4, 128], F32)
        zeta = mk("zeta", [128, 4], F32)
        gin = mk("gin", [128, 8], F32)
        gret = mk("gret", [128, 16], F32)
        ggla = mk("ggla", [128, 8], F32)
        bgt = mk("bgt", [128, 4], F32)
        negb = mk("negb", [128, 4], F32)
        fgain = mk("fgain", [128, D], F32)
        wgb = mk("wgb", [16, 512], BF16)
        cst = mk("cst", [128, 4], F32)
        Rret = mk("Rret", [128, 4, 2, 512], F32)
        Sret = mk("Sret", [128, 4, 2, 512], BF16)
        Rgla = mk("Rgla", [128, 4, 256], F32)
        Sgla = mk("Sgla", [128, 4, 256], BF16)
        ebl = mk("ebl", [128, 2, 4, 4], F32)
        uT = mk("uT", [128, 8, 512], BF16)
        xin = mk("xin", [128, 4, D], F32)
        utok = mk("utok", [128, 4, D], BF16)
        wsl_t = sb("wslot", [128, NS, 4096], BF16)
        csb = mk("csb", [128, 2, 2, 512], F32)
        AB = mk("AB", [128, 4, 512], F32)
        tmp = mk("tmp", [128, 4, 512], F32)
        sets_t = sb("sets", [128, 2, 8192], BF16)
        scm = mk("scm", [128, 4, 128], BF16)
        dg = mk("dg", [128, 4, 128], BF16)
        og = mk("og", [128, 2, 512], BF16)
        oTr = mk("oTr", [128, 16, 512], BF16)
        oTg = mk("oTg", [128, 8, 512], BF16)
        glrT = mk("glrT", [32, 512], BF16)
        mergedT = mk("mergedT", [128, 8, 512], BF16)
        xres = mk("xres", [128, 4, D], F32)
        stat = mk("stat", [128, 64], F32)
        junk = mk("junk", [128, 256], BF16)
        pj = mk("pj", [128, 2, 512], F32, psum=True, gran=512)
        sc = mk("sc", [128, 4, 128], F32, psum=True, gran=512)
        oacc = mk("oacc", [128, 2, 512], F32, psum=True, gran=512)
        stp = mk("stp", [128, 2, 512], F32, psum=True, gran=512)
        ptrf = mk("ptr", [128, 512], F32, psum=True, gran=512)
        ptr = Buf("ptr", ptrf.ap.bitcast(BF16), [128, 1024], gran=1024)
        ptr.rscale = 2
        scb = Buf("sc", sc.ap.rearrange("p a b -> p (a b)").bitcast(BF16), [128, 1024], gran=1024)
        scb.rscale = 2

        def wslot_view(slot, kc, w):
            ap = wsl_t[:, slot, 0:kc * w].rearrange("p (k w) -> p k w", k=kc)
            return Buf("wslot", ap, [128, kc, w], base=slot * 4096)

        def wstage_view(slot, kc, w):
            if slot < 4:
                ap = xres.ap[:, slot, 0:kc * w].rearrange("p (k w) -> p k w", k=kc)
                return Buf("xres", ap, [128, kc, w], base=slot * 1024)
            sl_ = slot - 3
            ap = xin.ap[:, sl_, 0:kc * w].rearrange("p (k w) -> p k w", k=kc)
            return Buf("xin", ap, [128, kc, w], base=sl_ * 1024)

        def set_view(si, off, shape):
            n = int(np.prod(shape[1:]))
            ap = sets_t[:, si, off:off + n]
            if len(shape) == 3:
                ap = ap.rearrange("p (a b) -> p a b", a=shape[1])
            return Buf("sets", ap, shape, base=si * 8192 + off)

        def rset(si):
            return dict(
                qT=set_view(si, 0, [128, 2, 512]),
                kT=set_view(si, 1024, [128, 2, 512]),
                qx=set_view(si, 2048, [128, 2, 512]),
                ktok=set_view(si, 3072, [128, 4, 256]),
                rv=set_view(si, 4096, [128, 4, 512]),
                sg=set_view(si, 6144, [128, 4, 512]),
            )

        def gset(si):
            return dict(
                qd=set_view(si, 0, [128, 2, 512]),
                ki=set_view(si, 1024, [128, 2, 512]),
                kitok=set_view(si, 2048, [128, 4, 256]),
                gv=set_view(si, 3072, [128, 4, 512]),
                sgg=set_view(si, 5120, [128, 4, 512]),
            )

        sigr_ = [set_view(m, 0, [128, 4, 512]) for m in range(2)]
        sigg_ = [set_view(m, 2048, [128, 4, 512]) for m in range(2)]

        dma_ctr = [0]

        def DMA(out_v, in_v, sem, q="sp", R=(), W=()):
            S.op(q, lambda e, o=out_v, i=in_v: e.dma_start(out=o, in_=i), R=R, W=W, dma=sem)

        ctx["pe_n"] = []

        def MM(out, lhsT, rhs, start, stop):
            ctx["pe_n"].append(rhs.ap.shape[-1])
            S.op("pe", lambda e, o=out.ap, l=lhsT.ap, r=rhs.ap, s0=start, s1=stop:
                 e.matmul(out=o, lhsT=l, rhs=r, start=s0, stop=s1),
                 R=[lhsT, rhs], W=[out])

        def TR(out, in_):
            ctx["pe_n"].append(128)
            S.op("pe", lambda e, o=out.ap, i=in_.ap, idn=ident().ap: e.transpose(out=o, in_=i, identity=idn),
                 R=[in_, ident()], W=[out])

        def ACT(out, in_, func, bias=None, scale=None, accum=None, extraR=()):
            kw = {}
            R = [in_] + list(extraR)
            W = [out]
            if bias is not None:
                if isinstance(bias, View):
                    kw["bias"] = bias.ap
                    R.append(bias)
                else:
                    kw["bias"] = bias
            if scale is not None:
                if isinstance(scale, View):
                    kw["scale"] = scale.ap
                    R.append(scale)
                else:
                    kw["scale"] = scale
            if accum is not None:
                kw["accum_out"] = accum.ap
                W.append(accum)
            S.op("act", lambda e, o=out.ap, i=in_.ap, f=func, k=kw: e.activation(out=o, in_=i, func=f, **k),
                 R=R, W=W)

        def TT(q, out, a, b, op):
            S.op(q, lambda e, o=out.ap, x=a.ap, y=b.ap, p=op: e.tensor_tensor(out=o, in0=x, in1=y, op=p),
                 R=[a, b], W=[out])

        def TS(q, out, a, s1, op0, s2=None, op1=None):
            R = [a]
            v1 = s1.ap if isinstance(s1, View) else s1
            v2 = s2.ap if isinstance(s2, View) else s2
            if isinstance(s1, View):
                R.append(s1)
            if isinstance(s2, View):
                R.append(s2)
            if op1 is None:
                S.op(q, lambda e, o=out.ap, x=a.ap: e.tensor_scalar(out=o, in0=x, scalar1=v1, scalar2=None, op0=op0),
                     R=R, W=[out])
            else:
                S.op(q, lambda e, o=out.ap, x=a.ap: e.tensor_scalar(out=o, in0=x, scalar1=v1, scalar2=v2,
                                                                     op0=op0, op1=op1),
                     R=R, W=[out])

        def STT(out, a, s, b, op0, op1):
            R = [a, b]
            sv = s.ap if isinstance(s, View) else s
            if isinstance(s, View):
                R.append(s)
            S.op("dve", lambda e, o=out.ap, x=a.ap, y=b.ap: e.scalar_tensor_tensor(
                out=o, in0=x, scalar=sv, in1=y, op0=op0, op1=op1), R=R, W=[out])

        def COPY(q, out, in_):
            if q == "act":
                ACT(out, in_, AF.Copy)
            else:
                S.op(q, lambda e, o=out.ap, i=in_.ap: e.tensor_copy(out=o, in_=i), R=[in_], W=[out])

        def MEMSET(q, out, val):
            S.op(q, lambda e, o=out.ap, v=val: e.memset(o, v), W=[out])

        for buf, src in [(ident, ident_d), (dmask, dmask_d), (causal, causal_d), (xi, xi_d), (zeta, zeta_d),
                         (gin, gin_d), (gret, gret_d), (ggla, ggla_d), (bgt, bg_d), (fgain, fg_d)]:
            key = tuple([slice(None)] * len(buf.shape))
            DMA(buf().ap, src[key], "c_" + buf.name, W=[buf()])
        MEMSET("pool", cst(slice(0, 1)), -0.5)
        MEMSET("pool", cst(slice(1, 2)), math.log(DKG ** -0.5))
        MEMSET("pool", cst(slice(2, 3)), EPS)
        MEMSET("pool", Rret(), 0.0)
        MEMSET("pool", Rgla(), 0.0)
        MEMSET("dve", Sret(), 0.0)
        MEMSET("dve", Sgla(), 0.0)
        MEMSET("dve", ebl(), 1.0)
        MEMSET("dve", glrT(), 0.0)
        TS("pool", negb(), bgt(), -1.0, ALU.mult, 0.0, ALU.add)
        wgs_v = tmp(0, p=(0, 16))
        DMA(wgs_v.ap, wg_d[:, :], "c_wgs", W=[wgs_v])
        COPY("dve", wgb(), wgs_v)
        mhalf = cst(slice(0, 1))
        lnsc = cst(slice(1, 2))
        epsc = cst(slice(2, 3))

        seq = [(b, c) for b in range(nblk) for c in order_c]
        loaded = [0]
        nextget = [0]
        conv_rr = [0]
        stage_rr = [0]
        src_aps = {"win": win_d, "wbr": wbr_d, "wbg": wbg_d, "wout": wout_d}
        gains = {"in": gin, "ret": gret, "gla": ggla}

        def load_chunk(i):
            b, c = seq[i]
            ch = chunks[c]
            slot = i % NS
            kc, w = ch["kc"], ch["w"]
            wv = wslot_view(slot, kc, w)
            if b == 0:
                for (src, c0, n, kc0, nkc, dst) in ch["pieces"]:
                    ss = stage_rr[0] % 7
                    stage_rr[0] += 1
                    sv = wstage_view(ss, nkc, n)
                    src_ap = src_aps[src][kc0 * 128:(kc0 + nkc) * 128, c0:c0 + n].rearrange("(k p) n -> p k n", p=128)
                    DMA(sv().ap, src_ap, "wst%d" % ss, W=[sv()])
                    q = ["dve", "pool", "act"][conv_rr[0] % 3]
                    conv_rr[0] += 1
                    dv = wv(slice(kc0, kc0 + nkc), slice(dst, dst + n))
                    if ch["gain"] is None:
                        COPY(q, dv, sv())
                    else:
                        g = gains[ch["gain"]]
                        if q == "act":
                            q = "dve"
                        gb = g(slice(kc0, kc0 + nkc))
                        gv_ = View(gb.ap.unsqueeze(2).to_broadcast([128, nkc, n]), gb.name, gb.rngs)
                        TT(q, dv, sv(), gv_, ALU.mult)
                DMA(scr_d[c, :, 0:kc * w], wsl_t[:, slot, 0:kc * w], "wsc%d" % slot,
                    R=[wv()], W=[raw(None, "scr", c * 4096, (c + 1) * 4096)])
            else:
                DMA(wsl_t[:, slot, 0:kc * w], scr_d[c, :, 0:kc * w], "wld%d" % slot,
                    R=[raw(None, "scr", c * 4096, (c + 1) * 4096)], W=[wv()])

        def ensure_loaded(hi):
            hi = min(len(seq), hi)
            while loaded[0] < hi:
                load_chunk(loaded[0])
                loaded[0] += 1

        def get_chunk(b, c, ahead=NS - 1):
            i = b * NCH + pos_of[c]
            assert i == nextget[0], (i, nextget[0], b, c)
            nextget[0] += 1
            ensure_loaded(i + ahead + 1)
            ch = chunks[c]
            return wslot_view(i % NS, ch["kc"], ch["w"])

        deferred = []

        def run_deferred():
            while deferred:
                deferred.pop(0)()

        xslot = [0]

        def N_dma(b):
            t0, nt = blocks[b]
            S.tag = "Ndma"
            for t in range(nt):
                if b == 0:
                    MEMSET("pool", xin(t), 0.0)
                    DMA(xin(t, p=(112, 128)).ap, meta_d[:, :], "xl%d" % t, W=[xin(t)])
                else:
                    r0 = (t0 - 1 + t) * 128
                    DMA(xin(t).ap, x_d[r0:r0 + 128, :], "xl%d" % t, W=[xin(t)])

        def N_pre(b, only=None):
            t0, nt = blocks[b]
            S.tag = "Npre"
            for t in range(nt):
                if only is not None and t != only:
                    continue
                ss = stat(slice(0 + t, 1 + t))
                ACT(utok(t), xin(t), AF.Square, accum=ss)
                t1 = stat(slice(4 + t, 5 + t))
                TS("pool", t1, ss, 1.0 / D, ALU.mult, EPS, ALU.add)
                rs = stat(slice(8 + t, 9 + t))
                TT("pool", rs, t1, mhalf, ALU.pow)
                if only is not None:
                    def late(t=t, rs=rs):
                        S.tag = "Npre"
                        TS("dve", utok(t), xin(t), rs, ALU.mult)
                    deferred.append(late)
                else:
                    TS("dve", utok(t), xin(t), rs, ALU.mult)

        def O_dma(b):
            t0, nt = blocks[b]
            if b == 0:
                return
            S.tag = "Odma"
            for t in range(nt):
                r0 = (t0 - 1 + t) * 128
                DMA(xres(t).ap, x_d[r0:r0 + 128, :], "xr%d" % t, W=[xres(t)])

        def N_post(b):
            t0, nt = blocks[b]
            S.tag = "N"
            for t in range(nt):
                pb = ptr if t % 2 == 0 else scb
                for kc in range(KC):
                    TR(pb(slice(kc * 128, (kc + 1) * 128)), utok(t, slice(kc * 128, (kc + 1) * 128)))
                pv = View(pb().ap.rearrange("p (k n) -> p k n", k=8), pb.name, [(0, 1024)])
                COPY("act", uT(slice(None), slice(t * 128, (t + 1) * 128)), pv)

        pj_rr = [0]

        def pj_next():
            pj_rr[0] += 1
            return pj_rr[0] % 2

        sc_rr = [0]
        oa_rr = [0]
        ptr_rr = [0]
        cs_slot = {}

        def load_cs(b):
            t0, nt = blocks[b]
            s = b % 2
            cs_slot[b] = s
            T = nt * 128
            DMA(csb(s, slice(None), slice(0, T)).ap, cs_d[:, :, t0 * 128:t0 * 128 + T], "cs%d" % s,
                W=[csb(s, slice(None), slice(0, T))])

        def P_ret(b, h):
            t0, nt = blocks[b]
            T = nt * 128
            rope_q = []

            def rope_emit(n):
                for _ in range(n):
                    if rope_q:
                        rope_q.pop(0)()

            rs_ = rset(h % 2)
            cs = cs_slot[b]
            cosv = csb(cs, 0, slice(0, T))
            sinv = csb(cs, 1, slice(0, T))
            w = get_chunk(b, C["qk", h])
            for ct in range(4):
                S.tag = "P_ret.qk"
                bk = pj_next()
                for kc in range(KC):
                    MM(pj(bk, slice(0, T)), w(kc, slice(ct * 128, (ct + 1) * 128)), uT(kc, slice(0, T)),
                       kc == 0, kc == KC - 1)
                COPY("act", AB(ct, slice(0, T)), pj(bk, slice(0, T)))
                if ct == 1 or ct == 3:
                    A_, B_ = AB(ct - 1, slice(0, T)), AB(ct, slice(0, T))
                    dst = rs_["qT"] if ct == 1 else rs_["kT"]
                    ts_ = (0, 1, 2, 3) if ct == 1 else (0, 1, 2, 3)
                    rope_q.append(lambda A_=A_: TT("dve", tmp(0, slice(0, T)), A_, cosv, ALU.mult))
                    rope_q.append(lambda B_=B_: TT("pool", tmp(1, slice(0, T)), B_, sinv, ALU.mult))
                    rope_q.append(lambda dst=dst: TT("dve", dst(0, slice(0, T)), tmp(0, slice(0, T)), tmp(1, slice(0, T)), ALU.subtract))
                    rope_q.append(lambda B_=B_: TT("pool", tmp(2, slice(0, T)), B_, cosv, ALU.mult))
                    rope_q.append(lambda A_=A_: TT("pool", tmp(3, slice(0, T)), A_, sinv, ALU.mult))
                    rope_q.append(lambda dst=dst: TT("pool", dst(1, slice(0, T)), tmp(2, slice(0, T)), tmp(3, slice(0, T)), ALU.add))
                    if ct == 1:
                        for hf in range(2):
                            def qxop(hf=hf):
                                qv = rs_["qT"](hf, slice(0, T))
                                qv3 = View(qv.ap.rearrange("p (t n) -> p t n", n=128), qv.name, qv.rngs)
                                xo = rs_["qx"](hf, slice(0, T))
                                xo3 = View(xo.ap.rearrange("p (t n) -> p t n", n=128), xo.name, xo.rngs)
                                xv = xi(h)
                                xb = View(xv.ap.unsqueeze(1).to_broadcast([128, nt, 128]), xv.name, xv.rngs)
                                TT("pool" if hf else "dve", xo3, qv3, xb, ALU.mult)
                            rope_q.append(qxop)
                    rope_emit(3 if h else 100)
                yield
            w = get_chunk(b, C["v", h])
            for t in range(nt):
                S.tag = "P_ret.v"
                bk = pj_next()
                for kc in range(KC):
                    MM(pj(bk), uT(kc, slice(t * 128, (t + 1) * 128)), w(kc), kc == 0, kc == KC - 1)
                COPY("act", rs_["rv"](t), pj(bk))
                S.tag = "P_ret.rope"
                rope_emit(3 if h else 100)
                yield
            w = get_chunk(b, C["g", h])

            def ktr_unit():
                rope_emit(100)
                S.tag = "P_ret.ktr"
                for t in range(nt):
                    for hf in range(2):
                        TR(scb(slice(t * 256 + hf * 128, t * 256 + (hf + 1) * 128)),
                           rs_["kT"](hf, slice(t * 128, (t + 1) * 128)))
                pv = scb(slice(0, nt * 256))
                kv = rs_["ktok"](slice(0, nt))
                kv2 = View(kv.ap.rearrange("p a b -> p (a b)"), kv.name, kv.rngs)
                ACT(kv2, pv, AF.Identity, scale=zeta(slice(h, h + 1)))

            for t in range(nt):
                if t == nt - 1:
                    ktr_unit()
                    yield
                S.tag = "P_ret.g"
                bk = pj_next()
                for kc in range(KC):
                    MM(pj(bk), uT(kc, slice(t * 128, (t + 1) * 128)), w(kc), kc == 0, kc == KC - 1)
                ACT(rs_["sg"](t), pj(bk), AF.Silu)
                S.tag = "P_ret.rope"
                rope_emit(3 if h else 100)
                yield

        def A_ret(b, h, pump):
            t0, nt = blocks[b]
            rs_ = rset(h % 2)
            sl = {}

            def scores(t):
                S.tag = "A_ret.sc"
                s = sc_rr[0] % 4
                sc_rr[0] += 1
                sl[t] = s
                for hf in range(2):
                    MM(sc(s), rs_["kT"](hf, slice(t * 128, (t + 1) * 128)),
                       rs_["qT"](hf, slice(t * 128, (t + 1) * 128)), hf == 0, hf == 1)
                TT("dve", scm(s), sc(s), dmask(h), ALU.mult)

            def omat(t):
                S.tag = "A_ret.o"
                s = sl[t]
                ob = oa_rr[0] % 2
                oa_rr[0] += 1
                for hf in range(2):
                    MM(stp(hf), rs_["ktok"](t, slice(hf * 128, (hf + 1) * 128)), rs_["rv"](t), True, True)
                MM(oacc(ob), scm(s), rs_["rv"](t), True, False)
                MM(oacc(ob), rs_["qx"](0, slice(t * 128, (t + 1) * 128)), Sret(h, 0), False, False)
                MM(oacc(ob), rs_["qx"](1, slice(t * 128, (t + 1) * 128)), Sret(h, 1), False, True)
                sb0 = 36 + ob * 12
                st6 = stat(slice(sb0, sb0 + 6))
                S.op("dve", lambda e, o=st6.ap, i=oacc(ob).ap: e.bn_stats(out=o, in_=i), R=[oacc(ob)], W=[st6])
                mv = stat(slice(sb0 + 6, sb0 + 8))
                S.op("dve", lambda e, o=mv.ap, i=st6.ap: e.bn_aggr(out=o, in_=i), R=[st6], W=[mv])
                ve = stat(slice(sb0 + 8, sb0 + 9))
                TS("pool", ve, stat(slice(sb0 + 7, sb0 + 8)), 1.0, ALU.mult, EPS, ALU.add)
                rstd = stat(slice(sb0 + 9, sb0 + 10))
                TT("pool", rstd, ve, mhalf, ALU.pow)
                TS("pool", dg(ob), ident(), rstd, ALU.mult, 0.0, ALU.add)
                Rv = Rret(h)
                Rv2 = View(Rv.ap.rearrange("p a b -> p (a b)"), Rv.name, Rv.rngs)
                Pv = stp()
                Pv2 = View(Pv.ap.rearrange("p a b -> p (a b)"), Pv.name, Pv.rngs)
                STT(Rv2, Rv2, GAMC[h], Pv2, ALU.mult, ALU.add)
                Sv = Sret(h)
                Sv2 = View(Sv.ap.rearrange("p a b -> p (a b)"), Sv.name, Sv.rngs)
                COPY("dve", Sv2, Rv2)
                STT(og(ob), oacc(ob), stat(slice(sb0 + 6, sb0 + 7)), rs_["sg"](t), ALU.subtract, ALU.mult)
                return ob

            def otr(t, ob):
                S.tag = "A_ret.otr"
                for ec in range(4):
                    MM(ptrf(slice(ec * 128, (ec + 1) * 128)), og(ob, slice(ec * 128, (ec + 1) * 128)), dg(ob), True, True)
                pv = ptrf()
                pv3 = View(pv.ap.rearrange("p (k n) -> p k n", k=4), pv.name, pv.rngs)
                COPY("act", oTr(slice(h * 4, h * 4 + 4), slice(t * 128, (t + 1) * 128)), pv3)

            scores(0)
            pump(1)
            run_deferred()
            obs = {}
            for t in range(nt):
                if t + 1 < nt:
                    scores(t + 1)
                pump(1)
                obs[t] = omat(t)
                pump(1)
                if t >= 1:
                    otr(t - 1, obs[t - 1])
                pump(1)
            deferred.append(lambda: otr(nt - 1, obs[nt - 1]))

        def P_gla(b, hp):
            t0, nt = blocks[b]
            T = nt * 128
            gs = gset(hp % 2)
            par = b % 2
            S.tag = "P_gla.glr"
            if hp == 0:
                w = get_chunk(b, C["glr"])
                bk = pj_next()
                for kc in range(KC):
                    MM(pj(bk, slice(0, T), p=(0, 16)), w(kc, slice(0, 16)), uT(kc, slice(0, T)), kc == 0, kc == KC - 1)
                COPY("act", glrT(slice(0, T), p=(0, 16)), pj(bk, slice(0, T), p=(0, 16)))
                yield
            for gl in range(2):
                g = 2 * hp + gl
                S.tag = "P_gla.gate"
                bk = pj_next()
                MM(pj(bk, slice(0, T)), wgb(slice(g * 128, (g + 1) * 128)), glrT(slice(0, T), p=(0, 16)), True, True)
                ACT(AB(0, slice(0, T)), pj(bk, slice(0, T)), AF.Exp, bias=negb(slice(g, g + 1)), scale=-1.0)
                ACT(AB(1, slice(0, T)), AB(0, slice(0, T)), AF.Ln, bias=1.0)
                for t in range(nt):
                    S.op("dve", lambda e, o=AB(2 + gl, slice(t * 128, (t + 1) * 128)).ap,
                         i=AB(1, slice(t * 128, (t + 1) * 128)).ap:
                         e.tensor_tensor_scan(out=o, data0=i, data1=i, initial=0.0, op0=ALU.add, op1=ALU.bypass),
                         R=[AB(1, slice(t * 128, (t + 1) * 128))], W=[AB(2 + gl, slice(t * 128, (t + 1) * 128))])
            yield
            w = get_chunk(b, C["gv", hp])
            for t in range(nt):
                S.tag = "P_gla.v"
                bk = pj_next()
                for kc in range(KC):
                    MM(pj(bk), uT(kc, slice(t * 128, (t + 1) * 128)), w(kc), kc == 0, kc == KC - 1)
                COPY("act", gs["gv"](t), pj(bk))
                yield
            S.tag = "P_gla.gate2"
            for gl in range(2):
                g = 2 * hp + gl
                csum = AB(2 + gl, slice(0, T))
                ACT(tmp(2 * gl, slice(0, T)), csum, AF.Exp, bias=lnsc, scale=-1.0 / TAU)
                ACT(tmp(2 * gl + 1, slice(0, T)), csum, AF.Exp, scale=1.0 / TAU)
                lastv = View(AB.ap[:, 2 + gl, 127:T:128], "AB", csum.rngs)
                ACT(ebl(par, g, slice(0, nt)), lastv, AF.Exp, scale=-1.0 / TAU)
            w = get_chunk(b, C["gg", hp])
            for t in range(nt):
                S.tag = "P_gla.g"
                bk = pj_next()
                for kc in range(KC):
                    MM(pj(bk), uT(kc, slice(t * 128, (t + 1) * 128)), w(kc), kc == 0, kc == KC - 1)
                ACT(gs["sgg"](t), pj(bk), AF.Silu)
                yield
            w = get_chunk(b, C["gqk", hp])
            for gl in range(2):
                S.tag = "P_gla.q"
                bk = pj_next()
                for kc in range(KC):
                    MM(pj(bk, slice(0, T)), w(kc, slice(gl * 128, (gl + 1) * 128)), uT(kc, slice(0, T)),
                       kc == 0, kc == KC - 1)
                TT("dve", gs["qd"](gl, slice(0, T)), pj(bk, slice(0, T)), tmp(2 * gl, slice(0, T)), ALU.mult)
                yield
                S.tag = "P_gla.k"
                bk = pj_next()
                for kc in range(KC):
                    MM(pj(bk, slice(0, T)), w(kc, slice(256 + gl * 128, 256 + (gl + 1) * 128)), uT(kc, slice(0, T)),
                       kc == 0, kc == KC - 1)
                TT("dve", gs["ki"](gl, slice(0, T)), pj(bk, slice(0, T)), tmp(2 * gl + 1, slice(0, T)), ALU.mult)
                yield
            S.tag = "P_gla.ktr"
            for t in range(nt):
                for gl in range(2):
                    TR(ptr(slice(t * 256 + gl * 128, t * 256 + (gl + 1) * 128)),
                       gs["ki"](gl, slice(t * 128, (t + 1) * 128)))
            pv = ptr(slice(0, nt * 256))
            kv = gs["kitok"](slice(0, nt))
            kv2 = View(kv.ap.rearrange("p a b -> p (a b)"), kv.name, kv.rngs)
            COPY("act", kv2, pv)
            yield

        og4 = Buf("og", og.ap.rearrange("p a b -> p (a b)").rearrange("p (k n) -> p k n", k=4), [128, 4, 256])
        gl_rr = [0]

        def A_gla(b, hp, pump, pump2=True):
            t0, nt = blocks[b]
            gs = gset(hp % 2)
            par = b % 2
            pend = []
            for t in range(nt):
                for gl in range(2):
                    g = 2 * hp + gl
                    S.tag = "A_gla.sc"
                    s = sc_rr[0] % 4
                    sc_rr[0] += 1
                    ts_ = slice(t * 128, (t + 1) * 128)
                    MM(sc(s), gs["ki"](gl, ts_), gs["qd"](gl, ts_), True, True)
                    TT("dve", scm(s), sc(s), causal(), ALU.mult)
                    ob = oa_rr[0] % 2
                    oa_rr[0] += 1
                    vv = gs["gv"](t, slice(gl * 256, (gl + 1) * 256))
                    MM(stp(gl, slice(0, 256)), gs["kitok"](t, slice(gl * 128, (gl + 1) * 128)), vv, True, True)
                    pump(1)
                    run_deferred()
                    S.tag = "A_gla.o"
                    MM(oacc(ob, slice(0, 256)), scm(s), vv, True, False)
                    MM(oacc(ob, slice(0, 256)), gs["qd"](gl, ts_), Sgla(g), False, True)
                    if t == 0:
                        if b == 0:
                            prev = ebl(1, g, slice(0, 1))
                        else:
                            pnt = blocks[b - 1][1]
                            prev = ebl(1 - par, g, slice(pnt - 1, pnt))
                    else:
                        prev = ebl(par, g, slice(t - 1, t))
                    STT(Rgla(g), Rgla(g), prev, stp(gl, slice(0, 256)), ALU.mult, ALU.add)
                    TS("dve", Sgla(g), Rgla(g), ebl(par, g, slice(t, t + 1)), ALU.mult)
                    ssq = stat(slice(20 + ob, 21 + ob))
                    ACT(junk(), oacc(ob, slice(0, 256)), AF.Square, accum=ssq)
                    t1 = stat(slice(22 + ob, 23 + ob))
                    TS("pool", t1, ssq, 1.0 / DVG, ALU.mult, EPS, ALU.add)
                    rstd = stat(slice(24 + ob, 25 + ob))
                    TT("pool", rstd, t1, mhalf, ALU.pow)
                    k4 = gl_rr[0] % 4
                    gl_rr[0] += 1
                    TS("pool", dg(k4), ident(), rstd, ALU.mult, 0.0, ALU.add)
                    TT("dve", og4(k4), oacc(ob, slice(0, 256)),
                       gs["sgg"](t, slice(gl * 256, (gl + 1) * 256)), ALU.mult)
                    pend.append((t, g, gl, k4))
                    if len(pend) > 2:
                        flush_gla(pend.pop(0), pend.pop(0), hp)
                    if pump2:
                        pump(1)
            assert len(pend) == 2
            p0, p1 = pend
            deferred.append(lambda: flush_gla(p0, p1, hp))

        def flush_gla(it0, it1, hp):
            S.tag = "A_gla.otr"
            assert it0[0] == it1[0]
            t = it0[0]
            for (t_, g, gl, k4) in (it0, it1):
                for ec in range(2):
                    MM(ptrf(slice((gl * 2 + ec) * 128, (gl * 2 + ec + 1) * 128)),
                       og4(k4, slice(ec * 128, (ec + 1) * 128)), dg(k4), True, True)
            pv = ptrf()
            pv3 = View(pv.ap.rearrange("p (k n) -> p k n", k=4), pv.name, pv.rngs)
            COPY("act", oTg(slice(hp * 4, hp * 4 + 4), slice(t * 128, (t + 1) * 128)), pv3)

        def P_mrg(b, mh):
            t0, nt = blocks[b]
            T = nt * 128
            for key, dst in (("mr", sigr_[mh]), ("mg", sigg_[mh])):
                w = get_chunk(b, C[key, mh])
                for n_ in range(4):
                    S.tag = "P_mrg." + key
                    bk = pj_next()
                    for kc in range(KC):
                        MM(pj(bk, slice(0, T)), w(kc, slice(n_ * 128, (n_ + 1) * 128)), uT(kc, slice(0, T)),
                           kc == 0, kc == KC - 1)
                    ACT(dst(n_, slice(0, T)), pj(bk, slice(0, T)), AF.Sigmoid)
                    yield

        def A_mrg(b, mh, pump):
            t0, nt = blocks[b]
            T = nt * 128
            sigr, sigg = sigr_[mh], sigg_[mh]
            run_deferred()
            S.tag = "A_mrg"
            for j in range(2):
                w = get_chunk(b, C["br", mh, j])
                for nl in range(2):
                    n_ = j * 2 + nl
                    ob = oa_rr[0] % 2
                    oa_rr[0] += 1
                    for ec in range(16):
                        MM(oacc(ob, slice(0, T)), w(ec, slice(nl * 128, (nl + 1) * 128)), oTr(ec, slice(0, T)),
                           ec == 0, ec == 15)
                    TT("dve", AB(n_, slice(0, T)), oacc(ob, slice(0, T)), sigr(n_, slice(0, T)), ALU.mult)
                    pump(1)
            w = get_chunk(b, C["bg", mh])
            for n_ in range(4):
                sb_ = n_ % 2
                for kc in range(KC):
                    MM(stp(sb_, slice(0, T)), w(kc, slice(n_ * 128, (n_ + 1) * 128)), oTg(kc, slice(0, T)),
                       kc == 0, kc == KC - 1)
                TT("dve", tmp(n_ % 2, slice(0, T)), stp(sb_, slice(0, T)), sigg(n_, slice(0, T)), ALU.mult)
                TT("pool", mergedT(mh * 4 + n_, slice(0, T)), tmp(n_ % 2, slice(0, T)), AB(n_, slice(0, T)), ALU.add)
                pump(1)

        oslot = [0]

        def stage_O(b):
            t0, nt = blocks[b]
            S.tag = "O"
            w0 = get_chunk(b, C["out", 0], ahead=1)
            w1 = get_chunk(b, C["out", 1], ahead=1)
            if b == 0:
                ensure_loaded(nextget[0] + NS)
                return

            def epilogue(t):
                s = t
                r0 = (t0 - 1 + t) * 128
                S.tag = "O.epi"
                ss = stat(slice(28 + s, 29 + s))
                onf = og()
                ACT(View(onf.ap.rearrange("p a b -> p (a b)"), onf.name, onf.rngs), xres(s), AF.Square, accum=ss)
                t1 = stat(slice(32 + s, 33 + s))
                TS("pool", t1, ss, 1.0 / D, ALU.mult, EPS, ALU.add)
                rs = stat(slice(60 + s, 61 + s))
                TT("pool", rs, t1, mhalf, ALU.pow)
                STT(xres(s), xres(s), rs, fgain(), ALU.mult, ALU.mult)
                DMA(out_d[r0:r0 + 128, :], xres(s).ap, "os%d" % s, R=[xres(s)])
                S.tag = "O"

            for t in range(nt):
                s = t
                r0 = (t0 - 1 + t) * 128
                for hf, w in ((0, w0), (1, w1)):
                    ob = oa_rr[0] % 2
                    oa_rr[0] += 1
                    for kc in range(KC):
                        MM(oacc(ob), mergedT(kc, slice(t * 128, (t + 1) * 128)), w(kc), kc == 0, kc == KC - 1)
                    xs = xres(s, slice(hf * 512, (hf + 1) * 512))
                    TT("dve", xs, oacc(ob), xs, ALU.add)
                if t == nt - 1:
                    ensure_loaded(nextget[0] + NS)
                if t >= 1:
                    epilogue(t - 1)
            epilogue(nt - 1)

        def drain(gen):
            if gen is not None:
                for _ in gen:
                    pass

        def mkpump(gen):
            def pump(n):
                if gen is None:
                    return
                for _ in range(n):
                    try:
                        next(gen)
                    except StopIteration:
                        return
            return pump

        N_dma(0)
        N_pre(0)
        N_post(0)
        ctx["pe_block_start"] = []
        for b in range(nblk):
            ctx["pe_block_start"].append(len(S.prog["pe"]))
            load_cs(b)
            if b >= 1 and b + 1 < nblk:
                N_dma(b + 1)
            stages = [("R", h) for h in range(HR)] + [("G", 0), ("G", 1)]
            Pf = {"R": P_ret, "G": P_gla, "M": P_mrg}
            Af = {"R": A_ret, "G": A_gla, "M": A_mrg}
            drain(Pf[stages[0][0]](b, stages[0][1]))
            for i, (k, a) in enumerate(stages):
                if i + 1 < len(stages):
                    gen = Pf[stages[i + 1][0]](b, stages[i + 1][1])
                else:
                    gen = P_mrg(b, 0)
                if k == "G" and a == 1:
                    Af[k](b, a, mkpump(gen), pump2=False)
                else:
                    Af[k](b, a, mkpump(gen))
                if i < 4 and b >= 1 and b + 1 < nblk:
                    N_pre(b + 1, only=i)
                drain(gen)
                if i == 0 and b >= 1:
                    O_dma(b)
            drain(P_mrg(b, 1))
            if b == 0 and b + 1 < nblk:
                N_dma(b + 1)
                N_pre(b + 1)
            A_mrg(b, 0, mkpump(None))
            A_mrg(b, 1, mkpump(None))
            if b + 1 < nblk:
                N_post(b + 1)
            stage_O(b)

        sem_names = sorted(S.cnt.keys())
        sems = {}
        for n in sem_names:
            sems[n] = es.enter_context(nc.semaphore("s_" + n))
        finals = [(n, S.cnt[n]) for n in sem_names if n.startswith("os")]
        block = es.enter_context(nc.Block())

        def runq(q):
            def f(e):
                for waits, fn, sem, inc, _tg in S.prog[q]:
                    for s, val in waits:
                        e.wait_ge(sems[s], val)
                    fn(e).then_inc(sems[sem], inc)
                if q == "sp":
                    for n, val in finals:
                        e.wait_ge(sems[n], val)
                    for n in ("pe", "act", "dve", "pool"):
                        if n in S.cnt:
                            e.wait_ge(sems[n], S.cnt[n])
            return f

        block.sync(runq("sp"))
        block.tensor(runq("pe"))
        block.scalar(runq("act"))
        block.vector(runq("dve"))
        block.gpsimd(runq("pool"))
    ctx["n_ops"] = S.n_ops
    ctx["tags"] = {q: [(p[4], len(p[0])) for p in S.prog[q]] for q in S.prog}
    return nc, ctx


def _consts(n_xtiles):
    n_tiles = n_xtiles + 1
    LP = n_tiles * 128
    half = 128
    inv = np.power(np.float32(10000.0), -(np.arange(half, dtype=np.float32) / np.float32(half))).astype(np.float32)
    pos = (np.arange(LP, dtype=np.float32) - np.float32(112.0)).astype(np.float32)
    ang = (pos[None, :] * inv[:, None]).astype(np.float32)
    cs = np.stack([np.cos(ang), np.sin(ang)], axis=1).astype(np.float32)
    ident = np.eye(128, dtype=np.float32).astype(ml_dtypes.bfloat16)
    idx = np.arange(128, dtype=np.float64)
    rel = idx[None, :] - idx[:, None]
    dmask = np.zeros((128, 4, 128), np.float32)
    xi = np.zeros((128, 4, 128), np.float32)
    zeta = np.zeros((128, 4), np.float32)
    for h in range(HR):
        dmask[:, h, :] = np.where(rel >= 0, np.exp(np.maximum(rel, 0) * LOGG[h]), 0.0) * (DKR ** -0.5)
        xi[:, h, :] = np.exp((idx + 1.0) * LOGG[h])[None, :]
        zeta[:, h] = np.exp((127.0 - idx) * LOGG[h]) * (DKR ** -0.5)
    causal = (rel >= 0).astype(np.float32)
    return dict(cs=cs, ident=ident, dmask=dmask, causal=causal, xi=xi, zeta=zeta)


_CACHE = {}


def run(inputs, n_xtiles):
    if n_xtiles not in _CACHE:
        _CACHE[n_xtiles] = (build_program(n_xtiles), _consts(n_xtiles))
    (nc, ctx), consts = _CACHE[n_xtiles]
    f = lambda a: np.ascontiguousarray(np.asarray(a), dtype=np.float32)
    x = f(inputs["x"])
    B = x.shape[0]
    shared = dict(
        meta=f(inputs["meta_tokens"]),
        w_in=f(inputs["w_in"])[0],
        w_bret=f(inputs["w_branch_ret"])[0],
        w_bgla=f(inputs["w_branch_gla"])[0],
        w_out=f(inputs["w_out"])[0],
        w_gate=f(inputs["w_gate_up"])[0],
        g_in=np.ascontiguousarray(f(inputs["norm_gain"])[0].reshape(8, 128).T),
        g_ret=np.ascontiguousarray(f(inputs["ret_norm_gain"])[0].reshape(16, 128).T),
        g_gla=np.ascontiguousarray(f(inputs["gla_norm_gain"])[0].reshape(8, 128).T),
        b_gate=np.ascontiguousarray(f(inputs["b_gate"])[0].reshape(4, 128).T),
        fgain=np.ascontiguousarray(np.broadcast_to(f(inputs["final_norm_gain"])[None, :], (128, D))),
    )
    shared.update(consts)
    in_maps = []
    for c in range(B):
        m = dict(shared)
        m["x"] = np.ascontiguousarray(x[c, :n_xtiles * 128])
        in_maps.append(m)
    res = run_bass_kernel_spmd(nc, in_maps, core_ids=list(range(B)))
    return np.stack([np.asarray(r["out"]) for r in res.results], axis=0).astype(np.float32)


def kernel(**inputs):
    return run(inputs, 64)
```

```python
import math
import numpy as np
import ml_dtypes
import concourse.bass as bass
import concourse.mybir as mybir
from concourse.bass_utils import run_bass_kernel_spmd

F32 = mybir.dt.float32
BF16 = mybir.dt.bfloat16
AF = mybir.ActivationFunctionType
ALU = mybir.AluOpType

D = 1024
KC = 8
N_META = 16
IN_COLS = 11280
HR, DKR, DVR = 4, 256, 512
HG, DKG, DVG = 4, 128, 256
EPS = 1e-6
TAU = 16.0
O_RQ, O_RK, O_RV, O_RG = 0, 1024, 2048, 4096
O_GQ, O_GK, O_GV, O_GG = 6144, 6656, 7168, 8192
O_GLR, O_MR, O_MG = 9216, 9232, 10256
GAM = [1.0 - 2.0 ** (-5.0 - h) for h in range(HR)]
LOGG = [math.log1p(-(2.0 ** (-5.0 - h))) for h in range(HR)]
GAMC = [math.exp(128.0 * lg) for lg in LOGG]
NS = 3


class View:
    __slots__ = ("ap", "name", "rngs")

    def __init__(self, ap, name, rngs):
        self.ap, self.name, self.rngs = ap, name, rngs


class Buf:
    def __init__(self, name, ap, shape, base=0, gran=1):
        self.name, self.ap, self.shape, self.base = name, ap, list(shape), base
        self.gran = gran
        self.f = self.shape[1:]
        st = [1] * len(self.f)
        for i in range(len(self.f) - 2, -1, -1):
            st[i] = st[i + 1] * self.f[i + 1]
        self.st = st

    def __call__(self, *idx, p=None):
        idx = list(idx) + [slice(None)] * (len(self.f) - len(idx))
        key = [slice(None) if p is None else slice(p[0], p[1])] + idx
        ap = self.ap[tuple(key)]
        dims = []
        for n, ix in zip(self.f, idx):
            if isinstance(ix, int):
                dims.append((ix, ix + 1))
            else:
                lo = 0 if ix.start is None else ix.start
                hi = n if ix.stop is None else ix.stop
                dims.append((lo, hi))
        k = len(dims) - 1
        while k > 0 and dims[k] == (0, self.f[k]):
            k -= 1
        rngs = []

        def rec(i, off):
            if i == k:
                rngs.append((off + dims[k][0] * self.st[k], off + dims[k][1] * self.st[k]))
                return
            for v in range(dims[i][0], dims[i][1]):
                rec(i + 1, off + v * self.st[i])

        rec(0, self.base)
        rs_ = getattr(self, "rscale", 1)
        if rs_ > 1:
            rngs = [(lo // rs_, -((-hi) // rs_)) for lo, hi in rngs]
        g = self.gran // rs_ if rs_ > 1 else self.gran
        if g > 1:
            rngs = [((lo // g) * g, -((-hi) // g) * g) for lo, hi in rngs]
        rngs.sort()
        m = [list(rngs[0])]
        for lo, hi in rngs[1:]:
            if lo <= m[-1][1]:
                m[-1][1] = max(m[-1][1], hi)
            else:
                m.append([lo, hi])
        return View(ap, self.name, [tuple(x) for x in m])


def raw(ap, name, lo, hi):
    return View(ap, name, [(lo, hi)])


class Sched:
    QUEUES = ["pe", "act", "dve", "pool", "sp"]

    def __init__(self):
        self.prog = {q: [] for q in self.QUEUES}
        self.cnt = {}
        self.waited = {q: {} for q in self.QUEUES}
        self.recs = {}
        self.n_ops = 0
        self.tag = ""

    PSUM_NAMES = ("pj", "sc", "oacc", "stp", "ptr")

    def _deps(self, v, write):
        out = []
        recs = self.recs.get(v.name, [])
        excl = v.name in self.PSUM_NAMES
        for lo, hi in v.rngs:
            for r in recs:
                if r[0] < hi and lo < r[1]:
                    if r[2] is not None:
                        out.append((r[2], "waw" if write else "raw"))
                    if write or excl:
                        for sem, (val, eng) in r[3].items():
                            out.append(((sem, val, eng), "war"))
        return out

    def _note(self, v, write, ev):
        recs = self.recs.setdefault(v.name, [])
        for lo, hi in v.rngs:
            if write:
                recs[:] = [r for r in recs if not (lo <= r[0] and r[1] <= hi)]
                recs.append([lo, hi, ev, {}])
            else:
                hit = None
                for r in recs:
                    if r[0] == lo and r[1] == hi:
                        hit = r
                        break
                if hit is None:
                    hit = [lo, hi, None, {}]
                    recs.append(hit)
                hit[3][ev[0]] = (ev[1], ev[2])

    def op(self, q, fn, R=(), W=(), dma=None):
        self.n_ops += 1
        if dma is not None:
            sem, inc, eng = dma, 16, "dma"
        else:
            sem, inc, eng = q, 1, q
        self.cnt[sem] = self.cnt.get(sem, 0) + inc
        ev = (sem, self.cnt[sem], eng)
        need = {}
        for v in R:
            for (s, val, e), kind in self._deps(v, False):
                if e == q and (q == "pe" or kind == "war"):
                    continue
                need[s] = max(need.get(s, 0), val)
        for v in W:
            for (s, val, e), kind in self._deps(v, True):
                if e == q and q == "pe":
                    continue
                need[s] = max(need.get(s, 0), val)
        waits = []
        wq = self.waited[q]
        for s, val in need.items():
            if wq.get(s, 0) >= val:
                continue
            wq[s] = val
            waits.append((s, val))
        self.prog[q].append((waits, fn, sem, inc, self.tag))
        for v in R:
            self._note(v, False, ev)
        for v in W:
            self._note(v, True, ev)
        return ev


def build_program(n_xtiles):
    assert n_xtiles % 4 == 0
    n_tiles = n_xtiles + 1
    blocks = [(0, 1)] + [(1 + 4 * i, 4) for i in range(n_xtiles // 4)]
    nblk = len(blocks)
    LP = n_tiles * 128

    nc = bass.Bass("TRN2", target_bir_lowering=False)
    dt_in = {}

    def din(name, shape, dt=F32):
        dt_in[name] = nc.dram_tensor(name, list(shape), dt, kind="ExternalInput").ap()
        return dt_in[name]

    x_d = din("x", [n_xtiles * 128, D])
    meta_d = din("meta", [N_META, D])
    win_d = din("w_in", [D, IN_COLS])
    wbr_d = din("w_bret", [2048, D])
    wbg_d = din("w_bgla", [D, D])
    wout_d = din("w_out", [D, D])
    wg_d = din("w_gate", [16, 512])
    gin_d = din("g_in", [128, 8])
    gret_d = din("g_ret", [128, 16])
    ggla_d = din("g_gla", [128, 8])
    bg_d = din("b_gate", [128, 4])
    fg_d = din("fgain", [128, D])
    cs_d = din("cs", [128, 2, LP])
    ident_d = din("ident", [128, 128], BF16)
    dmask_d = din("dmask", [128, 4, 128])
    causal_d = din("causal", [128, 128])
    xi_d = din("xi", [128, 4, 128])
    zeta_d = din("zeta", [128, 4])
    out_d = nc.dram_tensor("out", [n_xtiles * 128, D], F32, kind="ExternalOutput").ap()

    chunks = []

    def win_chunk(colsegs, gain="in"):
        w = sum(n for _, n in colsegs)
        pieces = []
        dst = 0
        for c0, n in colsegs:
            nkc = max(1, min(8, 1024 // n))
            for kc0 in range(0, 8, nkc):
                pieces.append(("win", c0, n, kc0, nkc, dst))
            dst += n
        chunks.append(dict(kc=8, w=w, pieces=pieces, gain=gain))
        return len(chunks) - 1

    C = {}
    for h in range(HR):
        C["qk", h] = win_chunk([(O_RQ + h * 256, 256), (O_RK + h * 256, 256)])
        C["v", h] = win_chunk([(O_RV + h * 512, 512)])
        C["g", h] = win_chunk([(O_RG + h * 512, 512)])
    C["glr"] = win_chunk([(O_GLR, 16)])
    for hp in range(2):
        g0, g1 = 2 * hp, 2 * hp + 1
        C["gqk", hp] = win_chunk([(O_GQ + g0 * 128, 128), (O_GQ + g1 * 128, 128),
                                  (O_GK + g0 * 128, 128), (O_GK + g1 * 128, 128)])
        C["gv", hp] = win_chunk([(O_GV + hp * 512, 512)])
        C["gg", hp] = win_chunk([(O_GG + hp * 512, 512)])
    for mh in range(2):
        C["mr", mh] = win_chunk([(O_MR + mh * 512, 512)])
        C["mg", mh] = win_chunk([(O_MG + mh * 512, 512)])
        for j in range(2):
            n0 = mh * 512 + j * 256
            chunks.append(dict(kc=16, w=256, gain="ret",
                               pieces=[("wbr", n0, 256, kc0, 4, 0) for kc0 in range(0, 16, 4)]))
            C["br", mh, j] = len(chunks) - 1
        chunks.append(dict(kc=8, w=512, gain="gla",
                           pieces=[("wbg", mh * 512, 512, kc0, 2, 0) for kc0 in range(0, 8, 2)]))
        C["bg", mh] = len(chunks) - 1
    for hf in range(2):
        chunks.append(dict(kc=8, w=512, gain=None,
                           pieces=[("wout", hf * 512, 512, kc0, 2, 0) for kc0 in range(0, 8, 2)]))
        C["out", hf] = len(chunks) - 1
    NCH = len(chunks)
    order = []
    for h in range(HR):
        order += [("qk", h), ("v", h), ("g", h)]
    order += ["glr"]
    for hp in range(2):
        order += [("gv", hp), ("gg", hp), ("gqk", hp)]
    order += [("mr", 0), ("mg", 0), ("mr", 1), ("mg", 1)]
    for mh in range(2):
        order += [("br", mh, 0), ("br", mh, 1), ("bg", mh)]
    order += [("out", 0), ("out", 1)]
    assert len(order) == NCH and len(set(order)) == NCH
    order_c = [C[k] for k in order]
    pos_of = {c: i for i, c in enumerate(order_c)}
    scr_d = nc.dram_tensor("wscr", [NCH, 128, 4096], BF16, kind="Internal").ap()

    S = Sched()
    ctx = {}

    import contextlib
    with contextlib.ExitStack() as es:
        def sb(name, shape, dt):
            t = es.enter_context(nc.sbuf_tensor("sb_" + name, list(shape), dt))
            return t

        def ps(name, shape, dt):
            t = es.enter_context(nc.psum_tensor("ps_" + name, list(shape), dt))
            return t

        def mk(name, shape, dt, psum=False, gran=1):
            t = (ps if psum else sb)(name, shape, dt)
            key = tuple([slice(None)] * len(shape))
            return Buf(name, t[key], shape, gran=gran)

        ident = mk("ident", [128, 128], BF16)
        dmask = mk("dmask", [128, 4, 128], F32)
        causal = mk("causal", [128, 128], F32)
        xi = mk("xi", [128, 4, 128], F32)
        zeta = mk("zeta", [128, 4], F32)
        gin = mk("gin", [128, 8], F32)
        gret = mk("gret", [128, 16], F32)
        ggla = mk("ggla", [128, 8], F32)
        bgt = mk("bgt", [128, 4], F32)
        negb = mk("negb", [128, 4], F32)
        fgain = mk("fgain", [128, D], F32)
        wgb = mk("wgb", [16, 512], BF16)
        cst = mk("cst", [128, 4], F32)
        Rret = mk("Rret", [128, 4, 2, 512], F32)
        Sret = mk("Sret", [128, 4, 2, 512], BF16)
        Rgla = mk("Rgla", [128, 4, 256], F32)
        Sgla = mk("Sgla", [128, 4, 256], BF16)
        ebl = mk("ebl", [128, 2, 4, 4], F32)
        uT = mk("uT", [128, 8, 512], BF16)
        xin = mk("xin", [128, 4, D], F32)
        utok = mk("utok", [128, 4, D], BF16)
        wsl_t = sb("wslot", [128, NS, 4096], BF16)
        csb = mk("csb", [128, 2, 2, 512], F32)
        AB = mk("AB", [128, 4, 512], F32)
        tmp = mk("tmp", [128, 4, 512], F32)
        sets_t = sb("sets", [128, 2, 8192], BF16)
        scm = mk("scm", [128, 4, 128], BF16)
        dg = mk("dg", [128, 4, 128], BF16)
        og = mk("og", [128, 2, 512], BF16)
        oTr = mk("oTr", [128, 16, 512], BF16)
        oTg = mk("oTg", [128, 8, 512], BF16)
        glrT = mk("glrT", [32, 512], BF16)
        mergedT = mk("mergedT", [128, 8, 512], BF16)
        xres = mk("xres", [128, 4, D], F32)
        stat = mk("stat", [128, 64], F32)
        junk = mk("junk", [128, 256], BF16)
        pj = mk("pj", [128, 2, 512], F32, psum=True, gran=512)
        sc = mk("sc", [128, 4, 128], F32, psum=True, gran=512)
        oacc = mk("oacc", [128, 2, 512], F32, psum=True, gran=512)
        stp = mk("stp", [128, 2, 512], F32, psum=True, gran=512)
        ptrf = mk("ptr", [128, 512], F32, psum=True, gran=512)
        ptr = Buf("ptr", ptrf.ap.bitcast(BF16), [128, 1024], gran=1024)
        ptr.rscale = 2
        scb = Buf("sc", sc.ap.rearrange("p a b -> p (a b)").bitcast(BF16), [128, 1024], gran=1024)
        scb.rscale = 2

        def wslot_view(slot, kc, w):
            ap = wsl_t[:, slot, 0:kc * w].rearrange("p (k w) -> p k w", k=kc)
            return Buf("wslot", ap, [128, kc, w], base=slot * 4096)

        def wstage_view(slot, kc, w):
            if slot < 4:
                ap = xres.ap[:, slot, 0:kc * w].rearrange("p (k w) -> p k w", k=kc)
                return Buf("xres", ap, [128, kc, w], base=slot * 1024)
            sl_ = slot - 3
            ap = xin.ap[:, sl_, 0:kc * w].rearrange("p (k w) -> p k w", k=kc)
            return Buf("xin", ap, [128, kc, w], base=sl_ * 1024)

        def set_view(si, off, shape):
            n = int(np.prod(shape[1:]))
            ap = sets_t[:, si, off:off + n]
            if len(shape) == 3:
                ap = ap.rearrange("p (a b) -> p a b", a=shape[1])
            return Buf("sets", ap, shape, base=si * 8192 + off)

        def rset(si):
            return dict(
                qT=set_view(si, 0, [128, 2, 512]),
                kT=set_view(si, 1024, [128, 2, 512]),
                qx=set_view(si, 2048, [128, 2, 512]),
                ktok=set_view(si, 3072, [128, 4, 256]),
                rv=set_view(si, 4096, [128, 4, 512]),
                sg=set_view(si, 6144, [128, 4, 512]),
            )

        def gset(si):
            return dict(
                qd=set_view(si, 0, [128, 2, 512]),
                ki=set_view(si, 1024, [128, 2, 512]),
                kitok=set_view(si, 2048, [128, 4, 256]),
                gv=set_view(si, 3072, [128, 4, 512]),
                sgg=set_view(si, 5120, [128, 4, 512]),
            )

        sigr_ = [set_view(m, 0, [128, 4, 512]) for m in range(2)]
        sigg_ = [set_view(m, 2048, [128, 4, 512]) for m in range(2)]

        dma_ctr = [0]

        def DMA(out_v, in_v, sem, q="sp", R=(), W=()):
            S.op(q, lambda e, o=out_v, i=in_v: e.dma_start(out=o, in_=i), R=R, W=W, dma=sem)

        ctx["pe_n"] = []

        def MM(out, lhsT, rhs, start, stop):
            ctx["pe_n"].append(rhs.ap.shape[-1])
            S.op("pe", lambda e, o=out.ap, l=lhsT.ap, r=rhs.ap, s0=start, s1=stop:
                 e.matmul(out=o, lhsT=l, rhs=r, start=s0, stop=s1),
                 R=[lhsT, rhs], W=[out])

        def TR(out, in_):
            ctx["pe_n"].append(128)
            S.op("pe", lambda e, o=out.ap, i=in_.ap, idn=ident().ap: e.transpose(out=o, in_=i, identity=idn),
                 R=[in_, ident()], W=[out])

        def ACT(out, in_, func, bias=None, scale=None, accum=None, extraR=()):
            kw = {}
            R = [in_] + list(extraR)
            W = [out]
            if bias is not None:
                if isinstance(bias, View):
                    kw["bias"] = bias.ap
                    R.append(bias)
                else:
                    kw["bias"] = bias
            if scale is not None:
                if isinstance(scale, View):
                    kw["scale"] = scale.ap
                    R.append(scale)
                else:
                    kw["scale"] = scale
            if accum is not None:
                kw["accum_out"] = accum.ap
                W.append(accum)
            S.op("act", lambda e, o=out.ap, i=in_.ap, f=func, k=kw: e.activation(out=o, in_=i, func=f, **k),
                 R=R, W=W)

        def TT(q, out, a, b, op):
            S.op(q, lambda e, o=out.ap, x=a.ap, y=b.ap, p=op: e.tensor_tensor(out=o, in0=x, in1=y, op=p),
                 R=[a, b], W=[out])

        def TS(q, out, a, s1, op0, s2=None, op1=None):
            R = [a]
            v1 = s1.ap if isinstance(s1, View) else s1
            v2 = s2.ap if isinstance(s2, View) else s2
            if isinstance(s1, View):
                R.append(s1)
            if isinstance(s2, View):
                R.append(s2)
            if op1 is None:
                S.op(q, lambda e, o=out.ap, x=a.ap: e.tensor_scalar(out=o, in0=x, scalar1=v1, scalar2=None, op0=op0),
                     R=R, W=[out])
            else:
                S.op(q, lambda e, o=out.ap, x=a.ap: e.tensor_scalar(out=o, in0=x, scalar1=v1, scalar2=v2,
                                                                     op0=op0, op1=op1),
                     R=R, W=[out])

        def STT(out, a, s, b, op0, op1):
            R = [a, b]
            sv = s.ap if isinstance(s, View) else s
            if isinstance(s, View):
                R.append(s)
            S.op("dve", lambda e, o=out.ap, x=a.ap, y=b.ap: e.scalar_tensor_tensor(
                out=o, in0=x, scalar=sv, in1=y, op0=op0, op1=op1), R=R, W=[out])

        def COPY(q, out, in_):
            if q == "act":
                ACT(out, in_, AF.Copy)
            else:
                S.op(q, lambda e, o=out.ap, i=in_.ap: e.tensor_copy(out=o, in_=i), R=[in_], W=[out])

        def MEMSET(q, out, val):
            S.op(q, lambda e, o=out.ap, v=val: e.memset(o, v), W=[out])

        for buf, src in [(ident, ident_d), (dmask, dmask_d), (causal, causal_d), (xi, xi_d), (zeta, zeta_d),
                         (gin, gin_d), (gret, gret_d), (ggla, ggla_d), (bgt, bg_d), (fgain, fg_d)]:
            key = tuple([slice(None)] * len(buf.shape))
            DMA(buf().ap, src[key], "c_" + buf.name, W=[buf()])
        MEMSET("pool", cst(slice(0, 1)), -0.5)
        MEMSET("pool", cst(slice(1, 2)), math.log(DKG ** -0.5))
        MEMSET("pool", cst(slice(2, 3)), EPS)
        MEMSET("pool", Rret(), 0.0)
        MEMSET("pool", Rgla(), 0.0)
        MEMSET("dve", Sret(), 0.0)
        MEMSET("dve", Sgla(), 0.0)
        MEMSET("dve", ebl(), 1.0)
        MEMSET("dve", glrT(), 0.0)
        TS("pool", negb(), bgt(), -1.0, ALU.mult, 0.0, ALU.add)
        wgs_v = tmp(0, p=(0, 16))
        DMA(wgs_v.ap, wg_d[:, :], "c_wgs", W=[wgs_v])
        COPY("dve", wgb(), wgs_v)
        mhalf = cst(slice(0, 1))
        lnsc = cst(slice(1, 2))
        epsc = cst(slice(2, 3))

        seq = [(b, c) for b in range(nblk) for c in order_c]
        loaded = [0]
        nextget = [0]
        conv_rr = [0]
        stage_rr = [0]
        src_aps = {"win": win_d, "wbr": wbr_d, "wbg": wbg_d, "wout": wout_d}
        gains = {"in": gin, "ret": gret, "gla": ggla}

        def load_chunk(i):
            b, c = seq[i]
            ch = chunks[c]
            slot = i % NS
            kc, w = ch["kc"], ch["w"]
            wv = wslot_view(slot, kc, w)
            if b == 0:
                for (src, c0, n, kc0, nkc, dst) in ch["pieces"]:
                    ss = stage_rr[0] % 7
                    stage_rr[0] += 1
                    sv = wstage_view(ss, nkc, n)
                    src_ap = src_aps[src][kc0 * 128:(kc0 + nkc) * 128, c0:c0 + n].rearrange("(k p) n -> p k n", p=128)
                    DMA(sv().ap, src_ap, "wst%d" % ss, W=[sv()])
                    q = ["dve", "pool", "act"][conv_rr[0] % 3]
                    conv_rr[0] += 1
                    dv = wv(slice(kc0, kc0 + nkc), slice(dst, dst + n))
                    if ch["gain"] is None:
                        COPY(q, dv, sv())
                    else:
                        g = gains[ch["gain"]]
                        if q == "act":
                            q = "dve"
                        gb = g(slice(kc0, kc0 + nkc))
                        gv_ = View(gb.ap.unsqueeze(2).to_broadcast([128, nkc, n]), gb.name, gb.rngs)
                        TT(q, dv, sv(), gv_, ALU.mult)
                DMA(scr_d[c, :, 0:kc * w], wsl_t[:, slot, 0:kc * w], "wsc%d" % slot,
                    R=[wv()], W=[raw(None, "scr", c * 4096, (c + 1) * 4096)])
            else:
                DMA(wsl_t[:, slot, 0:kc * w], scr_d[c, :, 0:kc * w], "wld%d" % slot,
                    R=[raw(None, "scr", c * 4096, (c + 1) * 4096)], W=[wv()])

        def ensure_loaded(hi):
            hi = min(len(seq), hi)
            while loaded[0] < hi:
                load_chunk(loaded[0])
                loaded[0] += 1

        def get_chunk(b, c, ahead=NS - 1):
            i = b * NCH + pos_of[c]
            assert i == nextget[0], (i, nextget[0], b, c)
            nextget[0] += 1
            ensure_loaded(i + ahead + 1)
            ch = chunks[c]
            return wslot_view(i % NS, ch["kc"], ch["w"])

        deferred = []

        def run_deferred():
            while deferred:
                deferred.pop(0)()

        xslot = [0]

        def N_dma(b):
            t0, nt = blocks[b]
            S.tag = "Ndma"
            for t in range(nt):
                if b == 0:
                    MEMSET("pool", xin(t), 0.0)
                    DMA(xin(t, p=(112, 128)).ap, meta_d[:, :], "xl%d" % t, W=[xin(t)])
                else:
                    r0 = (t0 - 1 + t) * 128
                    DMA(xin(t).ap, x_d[r0:r0 + 128, :], "xl%d" % t, W=[xin(t)])

        def N_pre(b, only=None):
            t0, nt = blocks[b]
            S.tag = "Npre"
            for t in range(nt):
                if only is not None and t != only:
                    continue
                ss = stat(slice(0 + t, 1 + t))
                ACT(utok(t), xin(t), AF.Square, accum=ss)
                t1 = stat(slice(4 + t, 5 + t))
                TS("pool", t1, ss, 1.0 / D, ALU.mult, EPS, ALU.add)
                rs = stat(slice(8 + t, 9 + t))
                TT("pool", rs, t1, mhalf, ALU.pow)
                if only is not None:
                    def late(t=t, rs=rs):
                        S.tag = "Npre"
                        TS("dve", utok(t), xin(t), rs, ALU.mult)
                    deferred.append(late)
                else:
                    TS("dve", utok(t), xin(t), rs, ALU.mult)

        def O_dma(b):
            t0, nt = blocks[b]
            if b == 0:
                return
            S.tag = "Odma"
            for t in range(nt):
                r0 = (t0 - 1 + t) * 128
                DMA(xres(t).ap, x_d[r0:r0 + 128, :], "xr%d" % t, W=[xres(t)])

        def N_post(b):
            t0, nt = blocks[b]
            S.tag = "N"
            for t in range(nt):
                pb = ptr if t % 2 == 0 else scb
                for kc in range(KC):
                    TR(pb(slice(kc * 128, (kc + 1) * 128)), utok(t, slice(kc * 128, (kc + 1) * 128)))
                pv = View(pb().ap.rearrange("p (k n) -> p k n", k=8), pb.name, [(0, 1024)])
                COPY("act", uT(slice(None), slice(t * 128, (t + 1) * 128)), pv)

        pj_rr = [0]

        def pj_next():
            pj_rr[0] += 1
            return pj_rr[0] % 2

        sc_rr = [0]
        oa_rr = [0]
        ptr_rr = [0]
        cs_slot = {}

        def load_cs(b):
            t0, nt = blocks[b]
            s = b % 2
            cs_slot[b] = s
            T = nt * 128
            DMA(csb(s, slice(None), slice(0, T)).ap, cs_d[:, :, t0 * 128:t0 * 128 + T], "cs%d" % s,
                W=[csb(s, slice(None), slice(0, T))])

        def P_ret(b, h):
            t0, nt = blocks[b]
            T = nt * 128
            rope_q = []

            def rope_emit(n):
                for _ in range(n):
                    if rope_q:
                        rope_q.pop(0)()

            rs_ = rset(h % 2)
            cs = cs_slot[b]
            cosv = csb(cs, 0, slice(0, T))
            sinv = csb(cs, 1, slice(0, T))
            w = get_chunk(b, C["qk", h])
            for ct in range(4):
                S.tag = "P_ret.qk"
                bk = pj_next()
                for kc in range(KC):
                    MM(pj(bk, slice(0, T)), w(kc, slice(ct * 128, (ct + 1) * 128)), uT(kc, slice(0, T)),
                       kc == 0, kc == KC - 1)
                COPY("act", AB(ct, slice(0, T)), pj(bk, slice(0, T)))
                if ct == 1 or ct == 3:
                    A_, B_ = AB(ct - 1, slice(0, T)), AB(ct, slice(0, T))
                    dst = rs_["qT"] if ct == 1 else rs_["kT"]
                    ts_ = (0, 1, 2, 3) if ct == 1 else (0, 1, 2, 3)
                    rope_q.append(lambda A_=A_: TT("dve", tmp(0, slice(0, T)), A_, cosv, ALU.mult))
                    rope_q.append(lambda B_=B_: TT("pool", tmp(1, slice(0, T)), B_, sinv, ALU.mult))
                    rope_q.append(lambda dst=dst: TT("dve", dst(0, slice(0, T)), tmp(0, slice(0, T)), tmp(1, slice(0, T)), ALU.subtract))
                    rope_q.append(lambda B_=B_: TT("pool", tmp(2, slice(0, T)), B_, cosv, ALU.mult))
                    rope_q.append(lambda A_=A_: TT("pool" if h else "dve", tmp(3, slice(0, T)), A_, sinv, ALU.mult))
                    rope_q.append(lambda dst=dst: TT("pool", dst(1, slice(0, T)), tmp(2, slice(0, T)), tmp(3, slice(0, T)), ALU.add))
                    if ct == 1:
                        for hf in range(2):
                            def qxop(hf=hf):
                                qv = rs_["qT"](hf, slice(0, T))
                                qv3 = View(qv.ap.rearrange("p (t n) -> p t n", n=128), qv.name, qv.rngs)
                                xo = rs_["qx"](hf, slice(0, T))
                                xo3 = View(xo.ap.rearrange("p (t n) -> p t n", n=128), xo.name, xo.rngs)
                                xv = xi(h)
                                xb = View(xv.ap.unsqueeze(1).to_broadcast([128, nt, 128]), xv.name, xv.rngs)
                                TT("pool" if hf else "dve", xo3, qv3, xb, ALU.mult)
                            rope_q.append(qxop)
                    rope_emit(3 if h else 100)
                yield
            w = get_chunk(b, C["v", h])
            for t in range(nt):
                S.tag = "P_ret.v"
                bk = pj_next()
                for kc in range(KC):
                    MM(pj(bk), uT(kc, slice(t * 128, (t + 1) * 128)), w(kc), kc == 0, kc == KC - 1)
                COPY("act", rs_["rv"](t), pj(bk))
                S.tag = "P_ret.rope"
                rope_emit(3 if h else 100)
                yield
            w = get_chunk(b, C["g", h])

            def ktr_unit():
                rope_emit(100)
                S.tag = "P_ret.ktr"
                for t in range(nt):
                    for hf in range(2):
                        TR(scb(slice(t * 256 + hf * 128, t * 256 + (hf + 1) * 128)),
                           rs_["kT"](hf, slice(t * 128, (t + 1) * 128)))
                pv = scb(slice(0, nt * 256))
                kv = rs_["ktok"](slice(0, nt))
                kv2 = View(kv.ap.rearrange("p a b -> p (a b)"), kv.name, kv.rngs)
                ACT(kv2, pv, AF.Identity, scale=zeta(slice(h, h + 1)))

            for t in range(nt):
                if t == nt - 1:
                    ktr_unit()
                    yield
                S.tag = "P_ret.g"
                bk = pj_next()
                for kc in range(KC):
                    MM(pj(bk), uT(kc, slice(t * 128, (t + 1) * 128)), w(kc), kc == 0, kc == KC - 1)
                ACT(rs_["sg"](t), pj(bk), AF.Silu)
                S.tag = "P_ret.rope"
                rope_emit(3 if h else 100)
                yield

        def A_ret(b, h, pump):
            t0, nt = blocks[b]
            rs_ = rset(h % 2)
            sl = {}

            def scores(t):
                S.tag = "A_ret.sc"
                s = sc_rr[0] % 4
                sc_rr[0] += 1
                sl[t] = s
                for hf in range(2):
                    MM(sc(s), rs_["kT"](hf, slice(t * 128, (t + 1) * 128)),
                       rs_["qT"](hf, slice(t * 128, (t + 1) * 128)), hf == 0, hf == 1)
                TT("dve", scm(s), sc(s), dmask(h), ALU.mult)

            def omat(t):
                S.tag = "A_ret.o"
                s = sl[t]
                ob = oa_rr[0] % 2
                oa_rr[0] += 1
                for hf in range(2):
                    MM(stp(hf), rs_["ktok"](t, slice(hf * 128, (hf + 1) * 128)), rs_["rv"](t), True, True)
                MM(oacc(ob), scm(s), rs_["rv"](t), True, False)
                MM(oacc(ob), rs_["qx"](0, slice(t * 128, (t + 1) * 128)), Sret(h, 0), False, False)
                MM(oacc(ob), rs_["qx"](1, slice(t * 128, (t + 1) * 128)), Sret(h, 1), False, True)
                sb0 = 36 + ob * 12
                st6 = stat(slice(sb0, sb0 + 6))
                S.op("dve", lambda e, o=st6.ap, i=oacc(ob).ap: e.bn_stats(out=o, in_=i), R=[oacc(ob)], W=[st6])
                mv = stat(slice(sb0 + 6, sb0 + 8))
                S.op("dve", lambda e, o=mv.ap, i=st6.ap: e.bn_aggr(out=o, in_=i), R=[st6], W=[mv])
                ve = stat(slice(sb0 + 8, sb0 + 9))
                TS("pool", ve, stat(slice(sb0 + 7, sb0 + 8)), 1.0, ALU.mult, EPS, ALU.add)
                rstd = stat(slice(sb0 + 9, sb0 + 10))
                TT("pool", rstd, ve, mhalf, ALU.pow)
                TS("pool", dg(ob), ident(), rstd, ALU.mult, 0.0, ALU.add)
                Rv = Rret(h)
                Rv2 = View(Rv.ap.rearrange("p a b -> p (a b)"), Rv.name, Rv.rngs)
                Pv = stp()
                Pv2 = View(Pv.ap.rearrange("p a b -> p (a b)"), Pv.name, Pv.rngs)
                STT(Rv2, Rv2, GAMC[h], Pv2, ALU.mult, ALU.add)
                Sv = Sret(h)
                Sv2 = View(Sv.ap.rearrange("p a b -> p (a b)"), Sv.name, Sv.rngs)
                COPY("dve", Sv2, Rv2)
                STT(og(ob), oacc(ob), stat(slice(sb0 + 6, sb0 + 7)), rs_["sg"](t), ALU.subtract, ALU.mult)
                return ob

            def otr(t, ob):
                S.tag = "A_ret.otr"
                for ec in range(4):
                    MM(ptrf(slice(ec * 128, (ec + 1) * 128)), og(ob, slice(ec * 128, (ec + 1) * 128)), dg(ob), True, True)
                pv = ptrf()
                pv3 = View(pv.ap.rearrange("p (k n) -> p k n", k=4), pv.name, pv.rngs)
                COPY("act", oTr(slice(h * 4, h * 4 + 4), slice(t * 128, (t + 1) * 128)), pv3)

            scores(0)
            pump(1)
            run_deferred()
            obs = {}
            for t in range(nt):
                if t + 1 < nt:
                    scores(t + 1)
                pump(1)
                obs[t] = omat(t)
                pump(1)
                if t >= 1:
                    otr(t - 1, obs[t - 1])
                pump(1)
            deferred.append(lambda: otr(nt - 1, obs[nt - 1]))

        def P_gla(b, hp):
            t0, nt = blocks[b]
            T = nt * 128
            gs = gset(hp % 2)
            par = b % 2
            S.tag = "P_gla.glr"
            if hp == 0:
                w = get_chunk(b, C["glr"])
                bk = pj_next()
                for kc in range(KC):
                    MM(pj(bk, slice(0, T), p=(0, 16)), w(kc, slice(0, 16)), uT(kc, slice(0, T)), kc == 0, kc == KC - 1)
                COPY("act", glrT(slice(0, T), p=(0, 16)), pj(bk, slice(0, T), p=(0, 16)))
                yield
            for gl in range(2):
                g = 2 * hp + gl
                S.tag = "P_gla.gate"
                bk = pj_next()
                MM(pj(bk, slice(0, T)), wgb(slice(g * 128, (g + 1) * 128)), glrT(slice(0, T), p=(0, 16)), True, True)
                ACT(AB(0, slice(0, T)), pj(bk, slice(0, T)), AF.Exp, bias=negb(slice(g, g + 1)), scale=-1.0)
                ACT(AB(1, slice(0, T)), AB(0, slice(0, T)), AF.Ln, bias=1.0)
                for t in range(nt):
                    S.op("dve", lambda e, o=AB(2 + gl, slice(t * 128, (t + 1) * 128)).ap,
                         i=AB(1, slice(t * 128, (t + 1) * 128)).ap:
                         e.tensor_tensor_scan(out=o, data0=i, data1=i, initial=0.0, op0=ALU.add, op1=ALU.bypass),
                         R=[AB(1, slice(t * 128, (t + 1) * 128))], W=[AB(2 + gl, slice(t * 128, (t + 1) * 128))])
            yield
            w = get_chunk(b, C["gv", hp])
            for t in range(nt):
                S.tag = "P_gla.v"
                bk = pj_next()
                for kc in range(KC):
                    MM(pj(bk), uT(kc, slice(t * 128, (t + 1) * 128)), w(kc), kc == 0, kc == KC - 1)
                COPY("act", gs["gv"](t), pj(bk))
                yield
            S.tag = "P_gla.gate2"
            for gl in range(2):
                g = 2 * hp + gl
                csum = AB(2 + gl, slice(0, T))
                ACT(tmp(2 * gl, slice(0, T)), csum, AF.Exp, bias=lnsc, scale=-1.0 / TAU)
                ACT(tmp(2 * gl + 1, slice(0, T)), csum, AF.Exp, scale=1.0 / TAU)
                lastv = View(AB.ap[:, 2 + gl, 127:T:128], "AB", csum.rngs)
                ACT(ebl(par, g, slice(0, nt)), lastv, AF.Exp, scale=-1.0 / TAU)
            w = get_chunk(b, C["gg", hp])
            for t in range(nt):
                S.tag = "P_gla.g"
                bk = pj_next()
                for kc in range(KC):
                    MM(pj(bk), uT(kc, slice(t * 128, (t + 1) * 128)), w(kc), kc == 0, kc == KC - 1)
                ACT(gs["sgg"](t), pj(bk), AF.Silu)
                yield
            w = get_chunk(b, C["gqk", hp])
            for gl in range(2):
                S.tag = "P_gla.q"
                bk = pj_next()
                for kc in range(KC):
                    MM(pj(bk, slice(0, T)), w(kc, slice(gl * 128, (gl + 1) * 128)), uT(kc, slice(0, T)),
                       kc == 0, kc == KC - 1)
                TT("dve", gs["qd"](gl, slice(0, T)), pj(bk, slice(0, T)), tmp(2 * gl, slice(0, T)), ALU.mult)
                yield
                S.tag = "P_gla.k"
                bk = pj_next()
                for kc in range(KC):
                    MM(pj(bk, slice(0, T)), w(kc, slice(256 + gl * 128, 256 + (gl + 1) * 128)), uT(kc, slice(0, T)),
                       kc == 0, kc == KC - 1)
                TT("dve", gs["ki"](gl, slice(0, T)), pj(bk, slice(0, T)), tmp(2 * gl + 1, slice(0, T)), ALU.mult)
                yield
            S.tag = "P_gla.ktr"
            for t in range(nt):
                for gl in range(2):
                    TR(ptr(slice(t * 256 + gl * 128, t * 256 + (gl + 1) * 128)),
                       gs["ki"](gl, slice(t * 128, (t + 1) * 128)))
            pv = ptr(slice(0, nt * 256))
            kv = gs["kitok"](slice(0, nt))
            kv2 = View(kv.ap.rearrange("p a b -> p (a b)"), kv.name, kv.rngs)
            COPY("act", kv2, pv)
            yield

        og4 = Buf("og", og.ap.rearrange("p a b -> p (a b)").rearrange("p (k n) -> p k n", k=4), [128, 4, 256])
        gl_rr = [0]

        def A_gla(b, hp, pump, pump2=True):
            t0, nt = blocks[b]
            gs = gset(hp % 2)
            par = b % 2
            pend = []
            for t in range(nt):
                for gl in range(2):
                    g = 2 * hp + gl
                    S.tag = "A_gla.sc"
                    s = sc_rr[0] % 4
                    sc_rr[0] += 1
                    ts_ = slice(t * 128, (t + 1) * 128)
                    MM(sc(s), gs["ki"](gl, ts_), gs["qd"](gl, ts_), True, True)
                    TT("dve", scm(s), sc(s), causal(), ALU.mult)
                    ob = oa_rr[0] % 2
                    oa_rr[0] += 1
                    vv = gs["gv"](t, slice(gl * 256, (gl + 1) * 256))
                    MM(stp(gl, slice(0, 256)), gs["kitok"](t, slice(gl * 128, (gl + 1) * 128)), vv, True, True)
                    pump(1)
                    run_deferred()
                    S.tag = "A_gla.o"
                    MM(oacc(ob, slice(0, 256)), scm(s), vv, True, False)
                    MM(oacc(ob, slice(0, 256)), gs["qd"](gl, ts_), Sgla(g), False, True)
                    if t == 0:
                        if b == 0:
                            prev = ebl(1, g, slice(0, 1))
                        else:
                            pnt = blocks[b - 1][1]
                            prev = ebl(1 - par, g, slice(pnt - 1, pnt))
                    else:
                        prev = ebl(par, g, slice(t - 1, t))
                    STT(Rgla(g), Rgla(g), prev, stp(gl, slice(0, 256)), ALU.mult, ALU.add)
                    TS("dve", Sgla(g), Rgla(g), ebl(par, g, slice(t, t + 1)), ALU.mult)
                    ssq = stat(slice(20 + ob, 21 + ob))
                    ACT(junk(), oacc(ob, slice(0, 256)), AF.Square, accum=ssq)
                    t1 = stat(slice(22 + ob, 23 + ob))
                    TS("pool", t1, ssq, 1.0 / DVG, ALU.mult, EPS, ALU.add)
                    rstd = stat(slice(24 + ob, 25 + ob))
                    TT("pool", rstd, t1, mhalf, ALU.pow)
                    k4 = gl_rr[0] % 4
                    gl_rr[0] += 1
                    TS("pool", dg(k4), ident(), rstd, ALU.mult, 0.0, ALU.add)
                    TT("dve", og4(k4), oacc(ob, slice(0, 256)),
                       gs["sgg"](t, slice(gl * 256, (gl + 1) * 256)), ALU.mult)
                    pend.append((t, g, gl, k4))
                    if len(pend) > 2:
                        flush_gla(pend.pop(0), pend.pop(0), hp)
                    if pump2:
                        pump(1)
            assert len(pend) == 2
            p0, p1 = pend
            deferred.append(lambda: flush_gla(p0, p1, hp))

        def flush_gla(it0, it1, hp):
            S.tag = "A_gla.otr"
            assert it0[0] == it1[0]
            t = it0[0]
            for (t_, g, gl, k4) in (it0, it1):
                for ec in range(2):
                    MM(ptrf(slice((gl * 2 + ec) * 128, (gl * 2 + ec + 1) * 128)),
                       og4(k4, slice(ec * 128, (ec + 1) * 128)), dg(k4), True, True)
            pv = ptrf()
            pv3 = View(pv.ap.rearrange("p (k n) -> p k n", k=4), pv.name, pv.rngs)
            COPY("act", oTg(slice(hp * 4, hp * 4 + 4), slice(t * 128, (t + 1) * 128)), pv3)

        def P_mrg(b, mh):
            t0, nt = blocks[b]
            T = nt * 128
            for key, dst in (("mr", sigr_[mh]), ("mg", sigg_[mh])):
                w = get_chunk(b, C[key, mh])
                for n_ in range(4):
                    S.tag = "P_mrg." + key
                    bk = pj_next()
                    for kc in range(KC):
                        MM(pj(bk, slice(0, T)), w(kc, slice(n_ * 128, (n_ + 1) * 128)), uT(kc, slice(0, T)),
                           kc == 0, kc == KC - 1)
                    ACT(dst(n_, slice(0, T)), pj(bk, slice(0, T)), AF.Sigmoid)
                    yield

        def A_mrg(b, mh, pump):
            t0, nt = blocks[b]
            T = nt * 128
            sigr, sigg = sigr_[mh], sigg_[mh]
            run_deferred()
            S.tag = "A_mrg"
            for j in range(2):
                w = get_chunk(b, C["br", mh, j])
                for nl in range(2):
                    n_ = j * 2 + nl
                    ob = oa_rr[0] % 2
                    oa_rr[0] += 1
                    for ec in range(16):
                        MM(oacc(ob, slice(0, T)), w(ec, slice(nl * 128, (nl + 1) * 128)), oTr(ec, slice(0, T)),
                           ec == 0, ec == 15)
                    TT("dve", AB(n_, slice(0, T)), oacc(ob, slice(0, T)), sigr(n_, slice(0, T)), ALU.mult)
                    pump(1)
            w = get_chunk(b, C["bg", mh])
            for n_ in range(4):
                sb_ = n_ % 2
                for kc in range(KC):
                    MM(stp(sb_, slice(0, T)), w(kc, slice(n_ * 128, (n_ + 1) * 128)), oTg(kc, slice(0, T)),
                       kc == 0, kc == KC - 1)
                TT("dve", tmp(n_ % 2, slice(0, T)), stp(sb_, slice(0, T)), sigg(n_, slice(0, T)), ALU.mult)
                TT("pool", mergedT(mh * 4 + n_, slice(0, T)), tmp(n_ % 2, slice(0, T)), AB(n_, slice(0, T)), ALU.add)
                pump(1)

        oslot = [0]

        def stage_O(b):
            t0, nt = blocks[b]
            S.tag = "O"
            w0 = get_chunk(b, C["out", 0], ahead=1)
            w1 = get_chunk(b, C["out", 1], ahead=1)
            if b == 0:
                ensure_loaded(nextget[0] + NS)
                return

            def epilogue(t):
                s = t
                r0 = (t0 - 1 + t) * 128
                S.tag = "O.epi"
                ss = stat(slice(28 + s, 29 + s))
                onf = og()
                ACT(View(onf.ap.rearrange("p a b -> p (a b)"), onf.name, onf.rngs), xres(s), AF.Square, accum=ss)
                t1 = stat(slice(32 + s, 33 + s))
                TS("pool", t1, ss, 1.0 / D, ALU.mult, EPS, ALU.add)
                rs = stat(slice(60 + s, 61 + s))
                TT("pool", rs, t1, mhalf, ALU.pow)
                STT(xres(s), xres(s), rs, fgain(), ALU.mult, ALU.mult)
                DMA(out_d[r0:r0 + 128, :], xres(s).ap, "os%d" % s, R=[xres(s)])
                S.tag = "O"

            for t in range(nt):
                s = t
                r0 = (t0 - 1 + t) * 128
                for hf, w in ((0, w0), (1, w1)):
                    ob = oa_rr[0] % 2
                    oa_rr[0] += 1
                    for kc in range(KC):
                        MM(oacc(ob), mergedT(kc, slice(t * 128, (t + 1) * 128)), w(kc), kc == 0, kc == KC - 1)
                    xs = xres(s, slice(hf * 512, (hf + 1) * 512))
                    TT("dve", xs, oacc(ob), xs, ALU.add)
                if t == nt - 1:
                    ensure_loaded(nextget[0] + NS)
                if t >= 1:
                    epilogue(t - 1)
            epilogue(nt - 1)

        def drain(gen):
            if gen is not None:
                for _ in gen:
                    pass

        def mkpump(gen):
            def pump(n):
                if gen is None:
                    return
                for _ in range(n):
                    try:
                        next(gen)
                    except StopIteration:
                        return
            return pump

        N_dma(0)
        N_pre(0)
        N_post(0)
        ctx["pe_block_start"] = []
        for b in range(nblk):
            ctx["pe_block_start"].append(len(S.prog["pe"]))
            load_cs(b)
            if b >= 1 and b + 1 < nblk:
                N_dma(b + 1)
            stages = [("R", h) for h in range(HR)] + [("G", 0), ("G", 1)]
            Pf = {"R": P_ret, "G": P_gla, "M": P_mrg}
            Af = {"R": A_ret, "G": A_gla, "M": A_mrg}
            drain(Pf[stages[0][0]](b, stages[0][1]))
            for i, (k, a) in enumerate(stages):
                if i + 1 < len(stages):
                    gen = Pf[stages[i + 1][0]](b, stages[i + 1][1])
                else:
                    gen = P_mrg(b, 0)
                if k == "G" and a == 1:
                    Af[k](b, a, mkpump(gen), pump2=False)
                else:
                    Af[k](b, a, mkpump(gen))
                if i < 4 and b >= 1 and b + 1 < nblk:
                    N_pre(b + 1, only=i)
                drain(gen)
                if i == 0 and b >= 1:
                    O_dma(b)
            drain(P_mrg(b, 1))
            if b == 0 and b + 1 < nblk:
                N_dma(b + 1)
                N_pre(b + 1)
            A_mrg(b, 0, mkpump(None))
            A_mrg(b, 1, mkpump(None))
            if b + 1 < nblk:
                N_post(b + 1)
            stage_O(b)

        sem_names = sorted(S.cnt.keys())
        sems = {}
        for n in sem_names:
            sems[n] = es.enter_context(nc.semaphore("s_" + n))
        finals = [(n, S.cnt[n]) for n in sem_names if n.startswith("os")]
        block = es.enter_context(nc.Block())

        def runq(q):
            def f(e):
                for waits, fn, sem, inc, _tg in S.prog[q]:
                    for s, val in waits:
                        e.wait_ge(sems[s], val)
                    fn(e).then_inc(sems[sem], inc)
                if q == "sp":
                    for n, val in finals:
                        e.wait_ge(sems[n], val)
                    for n in ("pe", "act", "dve", "pool"):
                        if n in S.cnt:
                            e.wait_ge(sems[n], S.cnt[n])
            return f

        block.sync(runq("sp"))
        block.tensor(runq("pe"))
        block.scalar(runq("act"))
        block.vector(runq("dve"))
        block.gpsimd(runq("pool"))
    ctx["n_ops"] = S.n_ops
    ctx["tags"] = {q: [(p[4], len(p[0])) for p in S.prog[q]] for q in S.prog}
    return nc, ctx


def _consts(n_xtiles):
    n_tiles = n_xtiles + 1
    LP = n_tiles * 128
    half = 128
    inv = np.power(np.float32(10000.0), -(np.arange(half, dtype=np.float32) / np.float32(half))).astype(np.float32)
    pos = (np.arange(LP, dtype=np.float32) - np.float32(112.0)).astype(np.float32)
    ang = (pos[None, :] * inv[:, None]).astype(np.float32)
    cs = np.stack([np.cos(ang), np.sin(ang)], axis=1).astype(np.float32)
    ident = np.eye(128, dtype=np.float32).astype(ml_dtypes.bfloat16)
    idx = np.arange(128, dtype=np.float64)
    rel = idx[None, :] - idx[:, None]
    dmask = np.zeros((128, 4, 128), np.float32)
    xi = np.zeros((128, 4, 128), np.float32)
    zeta = np.zeros((128, 4), np.float32)
    for h in range(HR):
        dmask[:, h, :] = np.where(rel >= 0, np.exp(np.maximum(rel, 0) * LOGG[h]), 0.0) * (DKR ** -0.5)
        xi[:, h, :] = np.exp((idx + 1.0) * LOGG[h])[None, :]
        zeta[:, h] = np.exp((127.0 - idx) * LOGG[h]) * (DKR ** -0.5)
    causal = (rel >= 0).astype(np.float32)
    return dict(cs=cs, ident=ident, dmask=dmask, causal=causal, xi=xi, zeta=zeta)


_CACHE = {}


def run(inputs, n_xtiles):
    if n_xtiles not in _CACHE:
        _CACHE[n_xtiles] = (build_program(n_xtiles), _consts(n_xtiles))
    (nc, ctx), consts = _CACHE[n_xtiles]
    f = lambda a: np.ascontiguousarray(np.asarray(a), dtype=np.float32)
    x = f(inputs["x"])
    B = x.shape[0]
    shared = dict(
        meta=f(inputs["meta_tokens"]),
        w_in=f(inputs["w_in"])[0],
        w_bret=f(inputs["w_branch_ret"])[0],
        w_bgla=f(inputs["w_branch_gla"])[0],
        w_out=f(inputs["w_out"])[0],
        w_gate=f(inputs["w_gate_up"])[0],
        g_in=np.ascontiguousarray(f(inputs["norm_gain"])[0].reshape(8, 128).T),
        g_ret=np.ascontiguousarray(f(inputs["ret_norm_gain"])[0].reshape(16, 128).T),
        g_gla=np.ascontiguousarray(f(inputs["gla_norm_gain"])[0].reshape(8, 128).T),
        b_gate=np.ascontiguousarray(f(inputs["b_gate"])[0].reshape(4, 128).T),
        fgain=np.ascontiguousarray(np.broadcast_to(f(inputs["final_norm_gain"])[None, :], (128, D))),
    )
    shared.update(consts)
    in_maps = []
    for c in range(B):
        m = dict(shared)
        m["x"] = np.ascontiguousarray(x[c, :n_xtiles * 128])
        in_maps.append(m)
    res = run_bass_kernel_spmd(nc, in_maps, core_ids=list(range(B)))
    return np.stack([np.asarray(r["out"]) for r in res.results], axis=0).astype(np.float32)


def kernel(**inputs):
    return run(inputs, 64)
```
